# Optimizing a Trainium2 kernel written in Bass

```python
import math
import jax, jax.numpy as jnp
from jax import lax
import numpy as np

D_MODEL = 2048
BATCH = 1
SEQ = 16384
DEPTH = 2

N_META = 16
BLOCK = 128
LEAD_PAD = BLOCK - N_META
WINDOW = 128

ATT_HEADS = 16
ATT_KV_HEADS = 4
ATT_HEAD_DIM = 64
ATT_GROUP = ATT_HEADS // ATT_KV_HEADS
REL_BUCKETS = 32
REL_MAX_DIST = 128

RET_HEADS = 8
RET_QK_DIM = 128
RET_V_DIM = 256

D_ATT = ATT_HEADS * ATT_HEAD_DIM
D_KV = ATT_KV_HEADS * ATT_HEAD_DIM
D_RQK = RET_HEADS * RET_QK_DIM
D_RV = RET_HEADS * RET_V_DIM
D_FF = -(-8 * D_MODEL // (3 * 256)) * 256

IN_SPLITS = [D_ATT, D_KV, D_KV, D_RQK, D_RQK, D_RV, D_RV, D_MODEL, D_MODEL]
D_IN = sum(IN_SPLITS)
IN_OFFSETS = [int(o) for o in np.cumsum(IN_SPLITS)[:-1]]

NORM_EPS = 1e-5
NEG = -1e30

kernel_name = "hybrid_swa_sink_retention_gated_block"


def rmsnorm(x, gain):
    xf = x.astype(jnp.float32)
    y = xf * lax.rsqrt(jnp.mean(xf * xf, axis=-1, keepdims=True) + NORM_EPS)
    return (y * gain.astype(jnp.float32)).astype(x.dtype)


def t5_bucket(rel):
    n = jnp.maximum(rel, 0)
    max_exact = REL_BUCKETS // 2
    large = max_exact + (jnp.log(jnp.maximum(n, 1).astype(jnp.float32) / max_exact)
                         / math.log(REL_MAX_DIST / max_exact) * (REL_BUCKETS - max_exact)).astype(jnp.int32)
    large = jnp.minimum(large, REL_BUCKETS - 1)
    return jnp.where(n < max_exact, n, large)


def attention_position_terms(rel_bias, n_blocks):
    i = jnp.arange(BLOCK, dtype=jnp.int32)
    j = jnp.arange(2 * BLOCK, dtype=jnp.int32)
    blk = jnp.arange(n_blocks, dtype=jnp.int32)
    q_pos = blk[:, None] * BLOCK + i[None, :]
    k_pos = (blk[:, None] - 1) * BLOCK + j[None, :]
    rel_band = i[:, None] + BLOCK - j[None, :]
    band_valid = ((k_pos[:, None, :] >= BLOCK)
                  & (rel_band[None] >= 0) & (rel_band[None] < WINDOW))
    band_bias = rel_bias[t5_bucket(rel_band)].astype(jnp.float32)
    band_bias = band_bias.transpose(2, 0, 1).reshape(ATT_KV_HEADS, ATT_GROUP, BLOCK, 2 * BLOCK)
    meta_pos = LEAD_PAD + jnp.arange(N_META, dtype=jnp.int32)
    rel_meta = q_pos[:, :, None] - meta_pos[None, None, :]
    meta_valid = rel_meta >= 0
    meta_bias = rel_bias[t5_bucket(rel_meta)].astype(jnp.float32)
    meta_bias = meta_bias.transpose(0, 3, 1, 2).reshape(n_blocks, ATT_KV_HEADS, ATT_GROUP, BLOCK, N_META)
    return band_bias, band_valid, meta_bias, meta_valid


def sliding_window_attention(q, k, v, sinks, band_bias, band_valid, meta_bias, meta_valid):
    B, Lp = q.shape[:2]
    N = Lp // BLOCK
    dtype = q.dtype
    qb = q.reshape(B, N, BLOCK, ATT_KV_HEADS, ATT_GROUP, ATT_HEAD_DIM)
    kb = k.reshape(B, N, BLOCK, ATT_KV_HEADS, ATT_HEAD_DIM)
    vb = v.reshape(B, N, BLOCK, ATT_KV_HEADS, ATT_HEAD_DIM)
    zero = jnp.zeros_like(kb[:, :1])
    k_band = jnp.concatenate([jnp.concatenate([zero, kb[:, :-1]], axis=1), kb], axis=2)
    v_band = jnp.concatenate([jnp.concatenate([zero, vb[:, :-1]], axis=1), vb], axis=2)
    k4 = k.reshape(B, Lp, ATT_KV_HEADS, ATT_HEAD_DIM)
    v4 = v.reshape(B, Lp, ATT_KV_HEADS, ATT_HEAD_DIM)
    k_meta = k4[:, LEAD_PAD:BLOCK]
    v_meta = v4[:, LEAD_PAD:BLOCK]
    scale = ATT_HEAD_DIM ** -0.5
    s_band = jnp.einsum('bnihgd,bnjhd->bnhgij', qb, k_band).astype(jnp.float32) * scale
    s_meta = jnp.einsum('bnihgd,bmhd->bnhgim', qb, k_meta).astype(jnp.float32) * scale
    s_band = jnp.where(band_valid[None, :, None, None], s_band + band_bias, NEG)
    s_meta = jnp.where(meta_valid[None, :, None, None], s_meta + meta_bias[None], NEG)
    sink = sinks.astype(jnp.float32).reshape(1, 1, ATT_KV_HEADS, ATT_GROUP, 1, 1)
    m = jnp.maximum(jnp.maximum(s_band.max(-1, keepdims=True), s_meta.max(-1, keepdims=True)), sink)
    p_band = jnp.exp(s_band - m)
    p_meta = jnp.exp(s_meta - m)
    inv = 1.0 / (p_band.sum(-1, keepdims=True) + p_meta.sum(-1, keepdims=True) + jnp.exp(sink - m))
    o = (jnp.einsum('bnhgij,bnjhd->bnihgd', (p_band * inv).astype(dtype), v_band)
         + jnp.einsum('bnhgim,bmhd->bnihgd', (p_meta * inv).astype(dtype), v_meta))
    return o.reshape(B, Lp, D_ATT)


def rotate_every_two(x):
    x1 = x[..., ::2]
    x2 = x[..., 1::2]
    return jnp.stack([-x2, x1], axis=-1).reshape(x.shape)


def retention(q, k, v, g, valid, pos):
    B, Lp = q.shape[:2]
    N = Lp // BLOCK
    C = BLOCK
    dtype = q.dtype
    f32 = jnp.float32
    q = q.astype(f32).reshape(B, Lp, RET_HEADS, RET_QK_DIM)
    k = k.astype(f32).reshape(B, Lp, RET_HEADS, RET_QK_DIM)
    v = v.astype(f32).reshape(B, Lp, RET_HEADS, RET_V_DIM)
    angle = jnp.repeat(1.0 / (10000.0 ** jnp.linspace(0.0, 1.0, RET_QK_DIM // 2, dtype=f32)), 2)
    theta = pos.astype(f32)[:, None] * angle[None, :]
    sin = jnp.sin(theta)[None, :, None, :]
    cos = jnp.cos(theta)[None, :, None, :]
    q = q * cos + rotate_every_two(q) * sin
    k = (k * cos + rotate_every_two(k) * sin) * (RET_QK_DIM ** -0.5)
    vmask = valid[None, :, None, None]
    k = jnp.where(vmask, k, 0.0)
    v = jnp.where(vmask, v, 0.0)
    log_decay = jnp.log(1.0 - 2.0 ** (-5.0 - jnp.arange(RET_HEADS, dtype=f32)))
    qc = q.reshape(B, N, C, RET_HEADS, RET_QK_DIM)
    kc = k.reshape(B, N, C, RET_HEADS, RET_QK_DIM)
    vc = v.reshape(B, N, C, RET_HEADS, RET_V_DIM)
    idx = jnp.arange(C, dtype=f32)
    diff = idx[:, None] - idx[None, :]
    intra_decay = jnp.where(diff[None] >= 0,
                            jnp.exp(log_decay[:, None, None] * jnp.maximum(diff, 0.0)[None]), 0.0)
    s = jnp.einsum('bnihd,bnjhd->bnhij', qc, kc) * intra_decay
    o_intra = jnp.einsum('bnhij,bnjhe->bnihe', s, vc)
    k_w = kc * jnp.exp((C - 1 - idx)[:, None] * log_decay[None, :])[None, None, :, :, None]
    kv = jnp.einsum('bnjhd,bnjhe->bnhde', k_w, vc)
    chunk_decay = jnp.exp(log_decay * C)[None, :, None, None]

    def step(state, kv_n):
        return chunk_decay * state + kv_n, state

    _, s_before = lax.scan(step, jnp.zeros((B, RET_HEADS, RET_QK_DIM, RET_V_DIM), f32),
                           jnp.moveaxis(kv, 1, 0))
    s_before = jnp.moveaxis(s_before, 0, 1)
    q_w = qc * jnp.exp((idx + 1.0)[:, None] * log_decay[None, :])[None, None, :, :, None]
    o_cross = jnp.einsum('bnihd,bnhde->bnihe', q_w, s_before)
    o = (o_intra + o_cross).reshape(B, Lp, RET_HEADS, RET_V_DIM)
    o = o * lax.rsqrt(jnp.mean(o * o, axis=-1, keepdims=True) + NORM_EPS)
    o = o.reshape(B, Lp, D_RV) * jax.nn.silu(g.astype(f32))
    return o.astype(dtype)


def hybrid_layer(h, norm1, w_in, sinks, w_attn_br, w_ret_br, w_out, norm2, w_gate_up, w_down,
                 band_bias, band_valid, meta_bias, meta_valid, valid, pos):
    xn = rmsnorm(h, norm1)
    u = xn @ w_in
    aq, ak, av, rq, rk, rv, rg, gate_a, gate_r = jnp.split(u, IN_OFFSETS, axis=-1)
    a = sliding_window_attention(aq, ak, av, sinks, band_bias, band_valid, meta_bias, meta_valid) @ w_attn_br
    r = retention(rq, rk, rv, rg, valid, pos) @ w_ret_br
    merged = jax.nn.sigmoid(gate_a) * a + jax.nn.sigmoid(gate_r) * r
    h = h + merged @ w_out
    xn2 = rmsnorm(h, norm2)
    gu = xn2 @ w_gate_up
    g, up = jnp.split(gu, [D_FF], axis=-1)
    return h + (jax.nn.silu(g) * up) @ w_down


def setup_inputs(seed: int = 0) -> dict:
    key = jax.random.key(seed)
    ks = jax.random.split(key, 13)
    f32 = jnp.float32
    nrm = lambda k, shape, s: jax.random.normal(k, shape, f32) * s
    return {
        "x": nrm(ks[0], (BATCH, SEQ, D_MODEL), 1.0),
        "meta_tokens": nrm(ks[1], (N_META, D_MODEL), 1.0),
        "rel_bias": nrm(ks[2], (REL_BUCKETS, ATT_HEADS), 0.5),
        "norm1": 1.0 + nrm(ks[3], (DEPTH, D_MODEL), 0.02),
        "w_in": nrm(ks[4], (DEPTH, D_MODEL, D_IN), D_MODEL ** -0.5),
        "attn_sinks": nrm(ks[5], (DEPTH, ATT_HEADS), 1.0),
        "w_attn_br": nrm(ks[6], (DEPTH, D_ATT, D_MODEL), D_ATT ** -0.5),
        "w_ret_br": nrm(ks[7], (DEPTH, D_RV, D_MODEL), D_RV ** -0.5),
        "w_out": nrm(ks[8], (DEPTH, D_MODEL, D_MODEL), D_MODEL ** -0.5),
        "norm2": 1.0 + nrm(ks[9], (DEPTH, D_MODEL), 0.02),
        "w_gate_up": nrm(ks[10], (DEPTH, D_MODEL, 2 * D_FF), D_MODEL ** -0.5),
        "w_down": nrm(ks[11], (DEPTH, D_FF, D_MODEL), D_FF ** -0.5),
        "norm_f": 1.0 + nrm(ks[12], (D_MODEL,), 0.02),
    }


def reference(x, meta_tokens, rel_bias, norm1, w_in, attn_sinks, w_attn_br, w_ret_br, w_out,
              norm2, w_gate_up, w_down, norm_f):
    B, S, D = x.shape
    Lp = S + BLOCK
    n_blocks = Lp // BLOCK
    lead = jnp.zeros((B, LEAD_PAD, D), x.dtype)
    meta = jnp.broadcast_to(meta_tokens.astype(x.dtype)[None], (B, N_META, D))
    h = jnp.concatenate([lead, meta, x], axis=1)
    pos = jnp.arange(Lp, dtype=jnp.int32) - LEAD_PAD
    valid = pos >= 0
    band_bias, band_valid, meta_bias, meta_valid = attention_position_terms(rel_bias, n_blocks)
    for l in range(DEPTH):
        h = hybrid_layer(h, norm1[l], w_in[l], attn_sinks[l], w_attn_br[l], w_ret_br[l], w_out[l],
                         norm2[l], w_gate_up[l], w_down[l],
                         band_bias, band_valid, meta_bias, meta_valid, valid, pos)
    h = rmsnorm(h, norm_f)
    return h[:, BLOCK:]
```

```python
import math
from contextlib import ExitStack
import numpy as np
import concourse.bass as bass
import concourse.mybir as mybir
from concourse.bass_utils import run_bass_kernel_spmd

F32 = mybir.dt.float32
BF16 = mybir.dt.bfloat16
AF = mybir.ActivationFunctionType
ALU = mybir.AluOpType
AX = mybir.AxisListType

D = 2048
DIN = 11776
DFF = 5632
NCORES = 8
O_AQ, O_AK, O_AV, O_RQ, O_RK, O_RV, O_RG, O_GA, O_GR = 0, 1024, 1280, 1536, 2560, 3584, 5632, 7680, 9728
EPS = 1e-5
NEGM = -30000.0
RING = 4


class Res:
    __slots__ = ("name", "lw", "rd", "dsem", "dcnt")

    def __init__(self, name):
        self.name = name
        self.lw = None
        self.rd = {}
        self.dsem = None
        self.dcnt = 0


class Prog:
    ENG = ("pe", "act", "dve", "pool", "sp")

    def __init__(self, nc):
        self.nc = nc
        self.e = {"pe": nc.tensor, "act": nc.scalar, "dve": nc.vector, "pool": nc.gpsimd, "sp": nc.sync}
        self.sem = {k: nc.alloc_semaphore("s_" + k) for k in self.ENG}
        self.cnt = {k: 0 for k in self.ENG}
        self.known = {k: {} for k in self.ENG}
        self.barsem = nc.alloc_semaphore("s_bar")
        self.barcnt = 0
        self.resreg = {}
        self.dkeys = []

    def res(self, name):
        r = self.resreg.get(name)
        if r is None:
            r = Res(name)
            self.resreg[name] = r
        return r

    def _wait(self, e, ev):
        if ev is None:
            return
        sem, val = ev
        k = self.known[e]
        if k.get(id(sem), 0) >= val:
            return
        self.e[e].wait_ge(sem, val)
        k[id(sem)] = val

    def _deps(self, e, reads, writes, same_ok=False):
        own = self.sem[e]
        for r in reads:
            if r.lw is not None and not (same_ok and r.lw[0] is own):
                self._wait(e, r.lw)
        for w in writes:
            if w.lw is not None and not (same_ok and w.lw[0] is own):
                self._wait(e, w.lw)
            for ev in w.rd.values():
                if not (same_ok and ev[0] is own):
                    self._wait(e, ev)

    def _commit(self, ev, reads, writes):
        for r in reads:
            r.rd[id(ev[0])] = ev
        for w in writes:
            w.lw = ev
            w.rd = {}

    def op(self, e, fn, reads=(), writes=(), same_ok=False):
        self._deps(e, reads, writes, same_ok)
        ins = fn(self.e[e])
        self.cnt[e] += 1
        ins.then_inc(self.sem[e], 1)
        ev = (self.sem[e], self.cnt[e])
        self._commit(ev, reads, writes)
        return ev

    def dma(self, q, out, in_, reads=(), writes=(), key=None):
        self._deps(q, reads, writes)
        if key is None:
            key = writes[0] if writes else reads[0]
        if key.dsem is None:
            key.dsem = self.nc.alloc_semaphore("d_" + key.name)
            self.dkeys.append(key)
        ins = self.e[q].dma_start(out=out, in_=in_)
        ins.then_inc(key.dsem, 16)
        key.dcnt += 16
        ev = (key.dsem, key.dcnt)
        self._commit(ev, reads, writes)
        return ev

    def custom(self, e, fn, sem, val, reads=(), writes=()):
        self._deps(e, reads, writes)
        ins = fn(self.e[e])
        ins.then_inc(sem, 1)
        ev = (sem, val)
        self._commit(ev, reads, writes)
        return ev

    def barrier(self):
        for o in ("pe", "act", "dve"):
            if self.cnt[o]:
                self._wait("sp", (self.sem[o], self.cnt[o]))
        for k in self.dkeys:
            if k.dcnt and k.name[:2] != "w_":
                self._wait("sp", (k.dsem, k.dcnt))
        self.e["sp"].sem_inc(self.barsem, 1)
        self.barcnt += 1
        for o in ("pe", "act", "dve"):
            self.e[o].wait_ge(self.barsem, self.barcnt)
            kn = self.known[o]
            for o2 in ("pe", "act", "dve"):
                kn[id(self.sem[o2])] = self.cnt[o2]
            for k in self.dkeys:
                if k.name[:2] != "w_":
                    kn[id(k.dsem)] = k.dcnt

    def wait_all(self, e, ress):
        for r in ress:
            self._wait(e, r.lw)
            for ev in r.rd.values():
                self._wait(e, ev)


class StopBuild(Exception):
    pass


class Tile:
    __slots__ = ("t", "r")

    def __init__(self, t, r):
        self.t = t
        self.r = r


def build(NOWN, DEPTH, stop=99, debug=False, mixstop=None):
    NCH = NOWN + 1
    nc = bass.Bass("TRN2", target_bir_lowering=False)
    P = Prog(nc)

    def din(name, shape, dt=F32):
        return nc.dram_tensor(name, list(shape), dt, kind="ExternalInput").ap()

    xin = din("xin", [NCH * 128, D])
    wspec = (("w_in", D, DIN), ("w_attn_br", 1024, D), ("w_ret_br", D, D), ("w_out", D, D),
             ("w_gate_up", D, 2 * DFF), ("w_down", DFF, D))
    wsh_in = {nm: din(nm, [2 * K_ // NCORES, N_]) for (nm, K_, N_) in wspec}
    wsh_bf = {nm: nc.dram_tensor(nm + "_shb", [2 * K_ // NCORES, N_], BF16).ap() for (nm, K_, N_) in wspec}
    wfull = {nm: nc.dram_tensor(nm + "_full", [2 * K_, N_], BF16).ap() for (nm, K_, N_) in wspec}
    gains = din("gains", [5, 128, D])
    sinks = din("sinks", [128, 32])
    ident_d = din("ident", [128, 128])
    bband_d = din("bband", [128, 16 * 256])
    bmeta_d = din("bmeta", [128, 3 * 256])
    rot_d = din("rot", [NCH * 128, 512])
    dect_d = din("dect", [128, 1024])
    qdec_d = din("qdec", [128, 1024])
    misc_d = din("misc", [128, 128])
    y = nc.dram_tensor("y", [NOWN * 128, D], F32, kind="ExternalOutput").ap()

    skind = "ExternalOutput" if debug else "Internal"
    uS = nc.dram_tensor("uS", [NCH * 128, DIN], BF16, kind=skind).ap()
    hS = nc.dram_tensor("hS", [NCH * 128, D], F32, kind=skind).ap()
    artS = nc.dram_tensor("artS", [NCH * 128, 24 * 128], BF16, kind=skind).ap()
    mS = nc.dram_tensor("mS", [NCH * 128, 16 * 128], BF16, kind=skind).ap()
    hidS = nc.dram_tensor("hidS", [NCH * 128, 44 * 128], BF16, kind=skind).ap()
    cinL = nc.dram_tensor("cinL", [128, 2048], F32).ap()
    coutL = nc.dram_tensor("coutL", [NCORES * 128, 2048], F32).ap()
    cinK = nc.dram_tensor("cinK", [128, 512], BF16).ap()
    coutK = nc.dram_tensor("coutK", [NCORES * 128, 512], BF16).ap()

    ccsem = nc.alloc_semaphore("ccsem")
    cccnt = [0]
    ucnt = [0]

    def un(name):
        ucnt[0] += 1
        return "%s_%d" % (name, ucnt[0])

    def chk(name):
        if mixstop == name:
            raise StopBuild()

    def ptile(name, shape, dt):
        return Tile(nc.alloc_sbuf_tensor(name, list(shape), dt), P.res(name))

    identb = ptile("identb", [128, 128], BF16)
    ring = [ptile("w_ring%d" % i, [128, 16, 512], BF16) for i in range(RING)]
    pf = [Tile(nc.alloc_psum_tensor("pf%d" % i, [128, 512], F32), P.res("pf%d" % i)) for i in range(6)]
    pb = [Tile(nc.alloc_psum_tensor("pb%d" % i, [128, 1024], BF16), P.res("pb%d" % i)) for i in range(2)]

    P.dma("pool", identb.t[:], ident_d, writes=[identb.r])
    wres = {}
    for (nm, K_, N_) in wspec:
        rs_, rf_ = P.res("wsh_" + nm), P.res("w_full_" + nm)
        P.dma("pool", wsh_bf[nm], wsh_in[nm], writes=[rs_], key=rs_)
        wres[nm] = rf_
    for (nm, K_, N_) in wspec:
        cccnt[0] += 1
        P.custom("pool", lambda e: e.collective_compute("AllGather", ALU.bypass, replica_groups=[list(range(NCORES))],
                                                        ins=[wsh_bf[nm]], outs=[wfull[nm]]),
                 ccsem, cccnt[0], reads=[P.res("wsh_" + nm)], writes=[wres[nm]])
    w_in = wfull["w_in"].rearrange("(l k) n -> l k n", l=2)
    w_ab = wfull["w_attn_br"].rearrange("(l k) n -> l k n", l=2)
    w_rb = wfull["w_ret_br"].rearrange("(l k) n -> l k n", l=2)
    w_out = wfull["w_out"].rearrange("(l k) n -> l k n", l=2)
    w_gu = wfull["w_gate_up"].rearrange("(l k) n -> l k n", l=2)
    w_dn = wfull["w_down"].rearrange("(l k) n -> l k n", l=2)

    wlist = []
    for l in range(DEPTH):
        for s in range(23):
            wlist.append((w_in[l, :, s * 512:(s + 1) * 512], 16, "w_in"))
        for s in range(4):
            wlist.append((w_ab[l, :, s * 512:(s + 1) * 512], 8, "w_attn_br"))
            wlist.append((w_rb[l, :, s * 512:(s + 1) * 512], 16, "w_ret_br"))
        for s in range(4):
            wlist.append((w_out[l, :, s * 512:(s + 1) * 512], 16, "w_out"))
        for j in range(11):
            wlist.append((w_gu[l, :, j * 512:(j + 1) * 512], 16, "w_gate_up"))
            wlist.append((w_gu[l, :, DFF + j * 512:DFF + (j + 1) * 512], 16, "w_gate_up"))
        for s in range(4):
            for (k0, kc) in ((0, 16), (16, 16), (32, 12)):
                wlist.append((w_dn[l, k0 * 128:(k0 + kc) * 128, s * 512:(s + 1) * 512], kc, "w_down"))
    wstate = {"issued": 0, "used": 0}

    def w_issue_upto(i):
        while wstate["issued"] <= min(i, len(wlist) - 1):
            j = wstate["issued"]
            src, kc, wn = wlist[j]
            blk = ring[j % RING]
            P.dma("pool", blk.t[:, 0:kc, :], src.rearrange("(kc p) n -> p kc n", p=128), reads=[wres[wn]], writes=[blk.r])
            wstate["issued"] += 1

    def w_get(count):
        i = wstate["used"]
        w_issue_upto(i + RING - 1)
        wstate["used"] += count
        return [ring[(i + j) % RING] for j in range(count)]

    def w_next():
        return w_get(1)[0]

    pfrot = [0]

    def next_pf(k=6):
        i = pfrot[0] % k
        pfrot[0] += 1
        return pf[i]

    evrot = [0]

    def evac_copy(out_ap, out_r, in_ap, in_r):
        evrot[0] += 1
        if evrot[0] % 2:
            P.op("act", lambda e: e.activation(out_ap, in_ap, AF.Copy), reads=[in_r], writes=[out_r])
        else:
            P.op("dve", lambda e: e.tensor_copy(out_ap, in_ap), reads=[in_r], writes=[out_r])

    def transposes_to(dst_fn, src_ap_fn, src_r, nblk, dst_r_list=None):
        for b0 in range(0, nblk, 8):
            nb = min(8, nblk - b0)
            bank = pb[(b0 // 8) % 2]
            for i in range(nb):
                P.op("pe", lambda e: e.transpose(bank.t[:, i * 128:(i + 1) * 128], src_ap_fn(b0 + i), identb.t[:]),
                     reads=[src_r, identb.r], writes=[bank.r], same_ok=True)
            dst_ap, dst_r = dst_fn(b0, b0 + nb)
            evac_copy(dst_ap, dst_r, bank.t[:, 0:nb * 128].rearrange("p (k t) -> p k t", t=128), bank.r)

    def norm_phase(st, src, gain_idx, prefix):
        gain = Tile(st.enter_context(nc.sbuf_tensor(un(prefix + "gain"), [128, D], F32)), P.res("gain"))
        P.dma("sp", gain.t[:], gains[gain_idx], writes=[gain.r])
        hb = [Tile(st.enter_context(nc.sbuf_tensor(un(prefix + "hb%d" % i), [128, D], F32)), P.res("hb%d" % i)) for i in range(2)]
        xn = [Tile(st.enter_context(nc.sbuf_tensor(un(prefix + "xn%d" % i), [128, D], BF16)), P.res("xn%d" % i)) for i in range(2)]
        junk = Tile(st.enter_context(nc.sbuf_tensor(un(prefix + "junk"), [128, D], BF16)), P.res("junk"))
        stat = [Tile(st.enter_context(nc.sbuf_tensor(un(prefix + "stat%d" % i), [128, 4], F32)), P.res("stat%d" % i)) for i in range(2)]
        xT = [Tile(st.enter_context(nc.sbuf_tensor(un(prefix + "xT%d" % n), [128, 16, 128], BF16)), P.res("xT%d" % n)) for n in range(NCH)]
        for n in range(NCH):
            h = hb[n % 2]
            P.dma("sp", h.t[:], src[n * 128:(n + 1) * 128, :], writes=[h.r])
            s_ = stat[n % 2]
            x_ = xn[n % 2]
            P.op("act", lambda e: e.activation(junk.t[:], h.t[:], AF.Square, accum_out=s_.t[:, 0:1]),
                 reads=[h.r], writes=[junk.r, s_.r])
            P.op("act", lambda e: e.activation(s_.t[:, 1:2], s_.t[:, 0:1], AF.Sqrt, bias=EPS, scale=1.0 / D),
                 reads=[s_.r], writes=[s_.r])
            P.op("dve", lambda e: e.reciprocal(s_.t[:, 2:3], s_.t[:, 1:2]), reads=[s_.r], writes=[s_.r])
            P.op("dve", lambda e: e.scalar_tensor_tensor(x_.t[:], h.t[:], s_.t[:, 2:3], gain.t[:], ALU.mult, ALU.mult),
                 reads=[h.r, s_.r, gain.r], writes=[x_.r])
            transposes_to(lambda lo, hi: (xT[n].t[:, lo:hi, :], xT[n].r),
                          lambda b: x_.t[:, b * 128:(b + 1) * 128], x_.r, 16)
        return xT

    def inproj_phase(st, l, xT):
        ust = [Tile(st.enter_context(nc.sbuf_tensor(un("ip_ust%d" % i), [128, 512], BF16)), P.res("st%d" % i)) for i in range(4)]
        k = 0
        for s in range(23):
            blk = w_next()
            for n in range(NCH):
                ps = next_pf()
                for kc in range(16):
                    P.op("pe", lambda e: e.matmul(ps.t[:], xT[n].t[:, kc, :], blk.t[:, kc, :], start=(kc == 0), stop=(kc == 15)),
                         reads=[xT[n].r, blk.r], writes=[ps.r], same_ok=True)
                u = ust[k % 4]
                k += 1
                evac_copy(u.t[:], u.r, ps.t[:], ps.r)
                P.dma("sp", uS[n * 128:(n + 1) * 128, s * 512:(s + 1) * 512], u.t[:], reads=[u.r], writes=[Res("x")], key=u.r)

    def mixer_phase(st, l):
        def T(name, shape, dt, rname=None):
            return Tile(st.enter_context(nc.sbuf_tensor(un("mx_" + name), list(shape), dt)), P.res(rname or ("mx_" + name)))

        if mixstop == "Z":
            return
        bband = T("bband", [128, 16, 256], F32)
        bmeta = T("bmeta", [128, 3, 16, 16], F32)
        dect = T("dect", [128, 8, 128], F32)
        qdec = T("qdec", [128, 8, 128], F32)
        misc = T("misc", [128, 128], F32)
        sinkb = T("sinkb", [128, 32], F32)
        cres = P.res("mx_consts")
        for (t_, src) in ((bband, bband_d), (bmeta, bmeta_d), (dect, dect_d), (qdec, qdec_d), (misc, misc_d), (sinkb, sinks)):
            P.dma("sp", t_.t[:].rearrange("p a b -> p (a b)") if len(t_.t.shape) == 3 else
                  (t_.t[:].rearrange("p a b c -> p (a b c)") if len(t_.t.shape) == 4 else t_.t[:]), src, writes=[t_.r], key=cres)
        for t_ in (bband, bmeta, dect, qdec, misc, sinkb):
            t_.r.lw = (cres.dsem, cres.dcnt)
        if mixstop == "Y":
            return
        kdec = misc.t[:, 0:8]
        cmask = misc.t[:, 88:89]
        cdh = [math.exp(math.log(1.0 - 2.0 ** (-5 - h)) * 128.0) for h in range(8)]

        uA = T("uA", [128, 1536], BF16)
        uR = T("uR", [128, 6144], BF16)
        rotb = [T("rot%d" % i, [128, 512], F32) for i in range(2)]
        S = T("S", [128, 8, 256], F32)
        Sbf = T("Sbf", [128, 8, 256], BF16)
        t1 = T("t1", [128, 8, 128], F32)
        t2 = T("t2", [128, 8, 128], F32)
        qr = T("qr", [128, 8, 128], BF16)
        kr = T("kr", [128, 8, 128], BF16)
        kw = T("kw", [128, 8, 128], BF16)
        qT = T("qT", [128, 8, 128], BF16)
        qwT = T("qwT", [128, 8, 128], BF16)
        kT = T("kT", [128, 8, 128], BF16)
        sTs = [T("sTs%d" % i, [128, 4, 128], BF16) for i in range(2)]
        sg1 = T("sg", [128, 1024], F32)
        sg = [sg1, sg1]
        orr = [T("orr%d" % i, [128, 1024], BF16) for i in range(2)]
        rst = T("rst", [128, 32], F32)
        aqT = T("aqT", [64, 16, 128], BF16)
        kTd = [T("kTd%d" % i, [128, 4, 128], BF16) for i in range(2)]
        kTd0 = T("kTd0", [128, 4, 128], BF16)
        kTdh = T("kTdh", [128, 4, 128], BF16)
        vk = [T("vk%d" % i, [128, 256], BF16) for i in range(2)]
        vkh = T("vkh", [128, 256], BF16)
        vmeta = T("vmeta", [16, 256], BF16)
        sb = [T("sb%d" % i, [128, 4, 272], F32) for i in range(2)]
        pp = [T("pp%d" % i, [128, 4, 272], BF16) for i in range(2)]
        pT = [T("pT%d" % i, [128, 8, 128], BF16) for i in range(2)]
        pTm = [T("pTm%d" % i, [16, 4, 128], BF16) for i in range(2)]
        ast = T("ast", [128, 64], F32)
        oa = T("oa", [128, 16, 64], BF16)
        art1 = T("art", [128, 24, 128], BF16)
        art = [art1, art1]
        hk = [T("hk%d" % i, [128, 512], BF16) for i in range(2)]
        hacc = T("hacc", [128, 512], F32)
        hkv = T("hkv", [128, 512], BF16)

        rcinK, rcoutK, rcinL, rcoutL = P.res("cinK"), P.res("coutK"), P.res("cinL"), P.res("coutL")
        if mixstop != "A2":
            P.dma("sp", hkv.t[:], uS[(NCH - 1) * 128:NCH * 128, O_AK:O_AK + 512], writes=[hkv.r])
            P.dma("pool", cinK, hkv.t[:], reads=[hkv.r], writes=[rcinK], key=rcinK)
        if mixstop == "A1":
            return
        P.dma("sp", vmeta.t[:], uS[112:128, O_AV:O_AV + 256], writes=[vmeta.r])
        if mixstop in ("A", "A2"):
            return

        def rotary(src_ap3, cos_ap, ss_ap, dst):
            s4 = src_ap3.rearrange("p h (d two) -> p h d two", two=2)
            ss3 = ss_ap.rearrange("p (d two) -> p d two", two=2)
            t24 = t2.t[:].rearrange("p h (d two) -> p h d two", two=2)
            rd = dst["reads"]
            P.op("dve", lambda e: e.tensor_tensor(t1.t[:], src_ap3, cos_ap.unsqueeze(1).broadcast_to([128, 8, 128]), ALU.mult),
                 reads=rd, writes=[t1.r])
            P.op("dve", lambda e: e.tensor_tensor(t24[:, :, :, 0], s4[:, :, :, 1], ss3[:, :, 0].unsqueeze(1).broadcast_to([128, 8, 64]), ALU.mult),
                 reads=rd, writes=[t2.r])
            P.op("dve", lambda e: e.tensor_tensor(t24[:, :, :, 1], s4[:, :, :, 0], ss3[:, :, 1].unsqueeze(1).broadcast_to([128, 8, 64]), ALU.mult),
                 reads=rd, writes=[t2.r], same_ok=True)
            P.op("dve", lambda e: e.tensor_tensor(dst["t"].t[:], t1.t[:], t2.t[:], ALU.add), reads=[t1.r, t2.r], writes=[dst["t"].r])

        def state_update(Stile, v3, vr, halves=(0, 1)):
            for r in halves:
                banks = (pf[3], pf[4])
                for hh in range(4):
                    h = 4 * r + hh
                    bk = banks[hh // 2]
                    P.op("pe", lambda e: e.matmul(bk.t[:, (hh % 2) * 256:(hh % 2 + 1) * 256], kw.t[:, h, :], v3[:, h, :], start=True, stop=True),
                         reads=[kw.r, vr], writes=[bk.r], same_ok=True)
                for hh in range(4):
                    h = 4 * r + hh
                    bk = banks[hh // 2]
                    P.op("dve", lambda e: e.scalar_tensor_tensor(Stile.t[:, h, :], Stile.t[:, h, :], float(cdh[h]),
                                                                 bk.t[:, (hh % 2) * 256:(hh % 2 + 1) * 256], ALU.mult, ALU.add),
                         reads=[bk.r], writes=[Stile.r])

        def load_rot(n):
            rb = rotb[n % 2]
            P.dma("sp", rb.t[:], rot_d[n * 128:(n + 1) * 128, :], writes=[rb.r])
            return rb

        P.op("dve", lambda e: e.memset(S.t[:], 0.0), writes=[S.r])
        for n in range(1, NCH):
            u2 = uR
            P.dma("sp", u2.t[:, 1024:4096], uS[n * 128:(n + 1) * 128, O_RK:O_RK + 3072], writes=[u2.r])
            rb = load_rot(n)
            k3 = u2.t[:, 1024:2048].rearrange("p (h d) -> p h d", h=8)
            v3 = u2.t[:, 2048:4096].rearrange("p (h e) -> p h e", h=8)
            rotary(k3, rb.t[:, 256:384], rb.t[:, 384:512], {"t": kr, "reads": [u2.r, rb.r]})
            P.op("dve", lambda e: e.tensor_tensor(kw.t[:], kr.t[:], kdec.unsqueeze(2).broadcast_to([128, 8, 128]), ALU.mult),
                 reads=[kr.r, misc.r], writes=[kw.r])
            state_update(S, v3, u2.r)
        P.dma("pool", cinL, S.t[:].rearrange("p h e -> p (h e)"), reads=[S.r], writes=[rcinL], key=rcinL)
        if mixstop == "B":
            return
        for (ci, co, rci, rco) in ((cinK, coutK, rcinK, rcoutK), (cinL, coutL, rcinL, rcoutL)):
            cccnt[0] += 1
            P.custom("pool", lambda e: e.collective_compute("AllGather", ALU.bypass, replica_groups=[list(range(NCORES))],
                                                            ins=[ci], outs=[co]),
                     ccsem, cccnt[0], reads=[rci], writes=[rco])

        if mixstop == "C":
            P.wait_all("sp", [rcoutK, rcoutL])
            return
        def make_kTd(k_ap, k_r, dst):
            bank = pb[1]
            for g in range(4):
                P.op("pe", lambda e: e.transpose(bank.t[0:64, g * 128:(g + 1) * 128], k_ap[:, g * 64:(g + 1) * 64], identb.t[:]),
                     reads=[k_r, identb.r], writes=[bank.r], same_ok=True)
            evac_copy(dst.t[0:64, :, :], dst.r, bank.t[0:64, 0:512].rearrange("p (g t) -> p g t", t=128), bank.r)

        def attention(n, u, kprev, vprev_ap, vprev_r, kcur, vcur_ap, vcur_r):
            mv = 0 if n == 0 else (1 if n == 1 else 2)
            for i2 in range(2):
                bank = pb[i2]
                for hq in range(8):
                    h = 8 * i2 + hq
                    P.op("pe", lambda e: e.transpose(bank.t[0:64, hq * 128:(hq + 1) * 128], u.t[:, O_AQ + h * 64:O_AQ + (h + 1) * 64], identb.t[:]),
                         reads=[u.r, identb.r], writes=[bank.r], same_ok=True)
                evac_copy(aqT.t[:, 8 * i2:8 * i2 + 8, :], aqT.r, bank.t[0:64, :].rearrange("p (k t) -> p k t", t=128), bank.r)
            chk("D1")
            pfm = pf[2]
            for g in range(4):
                bX, bY = pf[0], pf[1]
                s_ = sb[g % 2]
                p_ = pp[g % 2]
                for hh in range(4):
                    h = 4 * g + hh
                    bk = bX if hh < 2 else bY
                    o0 = (hh % 2) * 256
                    P.op("pe", lambda e: e.matmul(bk.t[:, o0:o0 + 128], aqT.t[:, h, :], kprev.t[0:64, g, :], start=True, stop=True),
                         reads=[aqT.r, kprev.r], writes=[bk.r], same_ok=True)
                    P.op("pe", lambda e: e.matmul(bk.t[:, o0 + 128:o0 + 256], aqT.t[:, h, :], kcur.t[0:64, g, :], start=True, stop=True),
                         reads=[aqT.r, kcur.r], writes=[bk.r], same_ok=True)
                    P.op("pe", lambda e: e.matmul(pfm.t[:, h * 16:(h + 1) * 16], aqT.t[:, h, :], kTd0.t[0:64, g, 112:128], start=True, stop=True),
                         reads=[aqT.r, kTd0.r], writes=[pfm.r], same_ok=True)
                for i2, bk in enumerate((bX, bY)):
                    P.op("dve", lambda e: e.scalar_tensor_tensor(s_.t[:, 2 * i2:2 * i2 + 2, 0:256], bk.t[:].rearrange("p (a j) -> p a j", a=2), 0.125,
                                                                 bband.t[:, 4 * g + 2 * i2:4 * g + 2 * i2 + 2, :], ALU.mult, ALU.add),
                         reads=[bk.r, bband.r], writes=[s_.r], same_ok=(i2 == 1))
                P.op("dve", lambda e: e.scalar_tensor_tensor(s_.t[:, :, 256:272], pfm.t[:, 64 * g:64 * g + 64].rearrange("p (a m) -> p a m", a=4), 0.125,
                                                             bmeta.t[:, mv, 4 * g:4 * g + 4, :], ALU.mult, ALU.add),
                     reads=[pfm.r, bmeta.r], writes=[s_.r], same_ok=True)
                if n == 0:
                    P.op("dve", lambda e: e.tensor_scalar(s_.t[:, :, 0:256], s_.t[:, :, 0:256], NEGM, None, ALU.add), reads=[s_.r], writes=[s_.r])
                elif n == 1:
                    P.op("dve", lambda e: e.tensor_scalar(s_.t[:, :, 0:128], s_.t[:, :, 0:128], cmask, None, ALU.add), reads=[s_.r, misc.r], writes=[s_.r])
                chk("D2")
                P.op("dve", lambda e: e.tensor_reduce(ast.t[:, 4 * g:4 * g + 4], s_.t[:], AX.X, ALU.max), reads=[s_.r], writes=[ast.r])
                P.op("dve", lambda e: e.tensor_tensor(ast.t[:, 4 * g:4 * g + 4], ast.t[:, 4 * g:4 * g + 4], sinkb.t[:, 16 * l + 4 * g:16 * l + 4 * g + 4], ALU.max),
                     reads=[ast.r, sinkb.r], writes=[ast.r])
                P.op("dve", lambda e: e.tensor_scalar(ast.t[:, 16 + 4 * g:20 + 4 * g], ast.t[:, 4 * g:4 * g + 4], -1.0, None, ALU.mult),
                     reads=[ast.r], writes=[ast.r])
                for hh in range(4):
                    h = 4 * g + hh
                    P.op("act", lambda e: e.activation(p_.t[:, hh, :], s_.t[:, hh, :], AF.Exp, bias=ast.t[:, 16 + h:17 + h], scale=1.0,
                                                       accum_out=ast.t[:, 32 + h:33 + h]),
                         reads=[s_.r, ast.r], writes=[p_.r, ast.r], same_ok=(hh > 0))
                chk("D3")
                b0, b1 = pb[0], pb[1]
                for hh in range(4):
                    P.op("pe", lambda e: e.transpose(b0.t[:, hh * 128:(hh + 1) * 128], p_.t[:, hh, 0:128], identb.t[:]),
                         reads=[p_.r, identb.r], writes=[b0.r], same_ok=True)
                    P.op("pe", lambda e: e.transpose(b0.t[:, (4 + hh) * 128:(5 + hh) * 128], p_.t[:, hh, 128:256], identb.t[:]),
                         reads=[p_.r, identb.r], writes=[b0.r], same_ok=True)
                    P.op("pe", lambda e: e.transpose(b1.t[0:16, hh * 128:(hh + 1) * 128], p_.t[:, hh, 256:272], identb.t[:]),
                         reads=[p_.r, identb.r], writes=[b1.r], same_ok=True)
                pt_, ptm_ = pT[g % 2], pTm[g % 2]
                evac_copy(pt_.t[:], pt_.r, b0.t[:].rearrange("p (k t) -> p k t", t=128), b0.r)
                evac_copy(ptm_.t[:], ptm_.r, b1.t[0:16, 0:512].rearrange("p (k t) -> p k t", t=128), b1.r)
                chk("D4")
                for hh in range(4):
                    h = 4 * g + hh
                    ob = pf[3] if h < 8 else pf[4]
                    oo = (h % 8) * 64
                    P.op("pe", lambda e: e.matmul(ob.t[:, oo:oo + 64], pt_.t[:, hh, :], vprev_ap[:, g * 64:(g + 1) * 64], start=True, stop=False),
                         reads=[pt_.r, vprev_r], writes=[ob.r], same_ok=True)
                    P.op("pe", lambda e: e.matmul(ob.t[:, oo:oo + 64], pt_.t[:, 4 + hh, :], vcur_ap[:, g * 64:(g + 1) * 64], start=False, stop=False),
                         reads=[pt_.r, vcur_r], writes=[ob.r], same_ok=True)
                    P.op("pe", lambda e: e.matmul(ob.t[:, oo:oo + 64], ptm_.t[:, hh, :], vmeta.t[:, g * 64:(g + 1) * 64], start=False, stop=True),
                         reads=[ptm_.r, vmeta.r], writes=[ob.r], same_ok=True)
            chk("D5")
            P.op("dve", lambda e: e.tensor_tensor(ast.t[:, 48:64], sinkb.t[:, 16 * l:16 * l + 16], ast.t[:, 16:32], ALU.add),
                 reads=[ast.r, sinkb.r], writes=[ast.r])
            P.op("act", lambda e: e.activation(ast.t[:, 48:64], ast.t[:, 48:64], AF.Exp), reads=[ast.r], writes=[ast.r])
            P.op("dve", lambda e: e.tensor_tensor(ast.t[:, 48:64], ast.t[:, 48:64], ast.t[:, 32:48], ALU.add), reads=[ast.r], writes=[ast.r])
            P.op("dve", lambda e: e.reciprocal(ast.t[:, 48:64], ast.t[:, 48:64]), reads=[ast.r], writes=[ast.r])
            for i2, ob in enumerate((pf[3], pf[4])):
                P.op("dve", lambda e: e.tensor_tensor(oa.t[:, 8 * i2:8 * i2 + 8, :], ob.t[:].rearrange("p (h d) -> p h d", h=8),
                                                      ast.t[:, 48 + 8 * i2:56 + 8 * i2].unsqueeze(2).broadcast_to([128, 8, 64]), ALU.mult),
                     reads=[ob.r, ast.r], writes=[oa.r], same_ok=(i2 == 1))

        P.op("dve", lambda e: e.memset(S.t[:], 0.0), reads=[], writes=[S.r])
        P.op("dve", lambda e: e.memset(Sbf.t[:], 0.0), writes=[Sbf.r])
        for n in range(NCH):
            u = uA
            P.dma("sp", uA.t[:], uS[n * 128:(n + 1) * 128, 0:1536], writes=[uA.r])
            P.dma("sp", uR.t[:], uS[n * 128:(n + 1) * 128, 1536:7680], writes=[uR.r])
            rb = load_rot(n)
            a_ = art[n % 2]
            kc_ = kTd0 if n == 0 else kTd[n % 2]
            make_kTd(u.t[:, O_AK:O_AK + 256], u.r, kc_)
            vc_ = vk[n % 2]
            P.op("act", lambda e: e.activation(vc_.t[:], u.t[:, O_AV:O_AV + 256], AF.Copy), reads=[u.r], writes=[vc_.r])
            if n == 0:
                kp_, vp_ = kc_, vc_
            elif n == 1:
                for r_ in range(NCORES - 1):
                    hk_ = hk[r_ % 2]
                    P.dma("sp", hk_.t[:], coutK[r_ * 128:(r_ + 1) * 128, :], reads=[rcoutK], writes=[hk_.r])
                    if r_ == 0:
                        P.op("dve", lambda e: e.tensor_scalar(hacc.t[:], hk_.t[:], misc.t[:, 80:81], None, ALU.mult),
                             reads=[hk_.r, misc.r], writes=[hacc.r])
                    else:
                        P.op("dve", lambda e: e.scalar_tensor_tensor(hacc.t[:], hk_.t[:], misc.t[:, 80 + r_:81 + r_], hacc.t[:], ALU.mult, ALU.add),
                             reads=[hk_.r, misc.r], writes=[hacc.r])
                P.op("dve", lambda e: e.tensor_copy(hkv.t[:], hacc.t[:]), reads=[hacc.r], writes=[hkv.r])
                make_kTd(hkv.t[:, 0:256], hkv.r, kTdh)
                P.op("act", lambda e: e.activation(vkh.t[:], hkv.t[:, 256:512], AF.Copy), reads=[hkv.r], writes=[vkh.r])
                kp_, vp_ = kTdh, vkh
            else:
                kp_, vp_ = (kTd0 if n - 1 == 0 else kTd[(n - 1) % 2]), vk[(n - 1) % 2]
            attention(n, u, kp_, vp_.t[:], vp_.r, kc_, vc_.t[:], vc_.r)
            transposes_to(lambda lo, hi: (a_.t[:, lo:hi, :], a_.r),
                          lambda b: oa.t[:, 2 * b:2 * b + 2, :].rearrange("p a d -> p (a d)"), oa.r, 8)
            if mixstop == "D":
                return
            u = uR
            q3 = u.t[:, 0:1024].rearrange("p (h d) -> p h d", h=8)
            k3 = u.t[:, 1024:2048].rearrange("p (h d) -> p h d", h=8)
            v3 = u.t[:, 2048:4096].rearrange("p (h e) -> p h e", h=8)
            rotary(q3, rb.t[:, 0:128], rb.t[:, 128:256], {"t": qr, "reads": [u.r, rb.r]})
            rotary(k3, rb.t[:, 256:384], rb.t[:, 384:512], {"t": kr, "reads": [u.r, rb.r]})
            P.op("dve", lambda e: e.tensor_tensor(kw.t[:], kr.t[:], kdec.unsqueeze(2).broadcast_to([128, 8, 128]), ALU.mult),
                 reads=[kr.r, misc.r], writes=[kw.r])
            for i in range(8):
                P.op("pe", lambda e: e.transpose(pb[0].t[:, i * 128:(i + 1) * 128], qr.t[:, i, :], identb.t[:]),
                     reads=[qr.r, identb.r], writes=[pb[0].r], same_ok=True)
            for i in range(8):
                P.op("pe", lambda e: e.transpose(pb[1].t[:, i * 128:(i + 1) * 128], kr.t[:, i, :], identb.t[:]),
                     reads=[kr.r, identb.r], writes=[pb[1].r], same_ok=True)
            pb0v = pb[0].t[:].rearrange("p (k t) -> p k t", t=128)
            P.op("act", lambda e: e.activation(qT.t[:], pb0v, AF.Copy), reads=[pb[0].r], writes=[qT.r])
            P.op("dve", lambda e: e.tensor_tensor(qwT.t[:], pb0v, qdec.t[:], ALU.mult), reads=[pb[0].r, qdec.r], writes=[qwT.r])
            evac_copy(kT.t[:], kT.r, pb[1].t[:].rearrange("p (k t) -> p k t", t=128), pb[1].r)
            for r in range(2):
                sT_ = sTs[r]
                for hh in range(4):
                    h = 4 * r + hh
                    P.op("pe", lambda e: e.matmul(pf[2].t[:, hh * 128:(hh + 1) * 128], kT.t[:, h, :], qT.t[:, h, :], start=True, stop=True),
                         reads=[kT.r, qT.r], writes=[pf[2].r], same_ok=True)
                P.op("dve", lambda e: e.tensor_tensor(sT_.t[:], pf[2].t[:].rearrange("p (a i) -> p a i", a=4), dect.t[:, 4 * r:4 * r + 4, :], ALU.mult),
                     reads=[pf[2].r, dect.r], writes=[sT_.r])
                banks = (pf[0], pf[1])
                for hh in range(4):
                    h = 4 * r + hh
                    bk = banks[hh // 2]
                    o0 = (hh % 2) * 256
                    P.op("pe", lambda e: e.matmul(bk.t[:, o0:o0 + 256], sT_.t[:, hh, :], v3[:, h, :], start=True, stop=False),
                         reads=[sT_.r, u.r], writes=[bk.r], same_ok=True)
                    P.op("pe", lambda e: e.matmul(bk.t[:, o0:o0 + 256], qwT.t[:, h, :], Sbf.t[:, h, :], start=False, stop=True),
                         reads=[qwT.r, Sbf.r], writes=[bk.r], same_ok=True)
                for hh in range(4):
                    h = 4 * r + hh
                    bk = banks[hh // 2]
                    o0 = (hh % 2) * 256
                    P.op("act", lambda e: e.activation(t1.t[:, 0:2, :].rearrange("p a d -> p (a d)"), bk.t[:, o0:o0 + 256], AF.Square,
                                                       accum_out=rst.t[:, h:h + 1]),
                         reads=[bk.r], writes=[t1.r, rst.r], same_ok=(hh > 0))
                P.op("act", lambda e: e.activation(rst.t[:, 8 + 4 * r:12 + 4 * r], rst.t[:, 4 * r:4 * r + 4], AF.Sqrt, bias=EPS, scale=1.0 / 256),
                     reads=[rst.r], writes=[rst.r])
                P.op("dve", lambda e: e.reciprocal(rst.t[:, 16 + 4 * r:20 + 4 * r], rst.t[:, 8 + 4 * r:12 + 4 * r]), reads=[rst.r], writes=[rst.r])
                g_ = sg[r]
                P.op("act", lambda e: e.activation(g_.t[:], u.t[:, 4096 + 1024 * r:4096 + 1024 * (r + 1)], AF.Silu), reads=[u.r], writes=[g_.r])
                o_ = orr[r]
                for hh in range(4):
                    h = 4 * r + hh
                    bk = banks[hh // 2]
                    o0 = (hh % 2) * 256
                    P.op("dve", lambda e: e.scalar_tensor_tensor(o_.t[:, hh * 256:(hh + 1) * 256], bk.t[:, o0:o0 + 256], rst.t[:, 16 + h:17 + h],
                                                                 g_.t[:, hh * 256:(hh + 1) * 256], ALU.mult, ALU.mult),
                         reads=[bk.r, rst.r, g_.r], writes=[o_.r], same_ok=(hh > 0))
                transposes_to(lambda lo, hi: (a_.t[:, 8 + 8 * r + lo:8 + 8 * r + hi, :], a_.r),
                              lambda b: o_.t[:, b * 128:(b + 1) * 128], o_.r, 8)
            if mixstop == "E":
                return
            state_update(S, v3, u.r)
            if n == 0:
                for h in range(8):
                    P.op("dve", lambda e: e.tensor_scalar(S.t[:, h, :], S.t[:, h, :], misc.t[:, 8 + h:9 + h], None, ALU.mult),
                         reads=[S.r, misc.r], writes=[S.r])
                Gv = uR.t[:].bitcast(F32)
                for c_ in range(NCORES - 1):
                    P.dma("sp", Gv[:, 0:2048], coutL[c_ * 128:(c_ + 1) * 128, :], reads=[rcoutL], writes=[uR.r])
                    for h in range(8):
                        P.op("dve", lambda e: e.scalar_tensor_tensor(S.t[:, h, :], Gv[:, h * 256:(h + 1) * 256], misc.t[:, 16 + 8 * c_ + h:17 + 8 * c_ + h],
                                                                     S.t[:, h, :], ALU.mult, ALU.add),
                             reads=[uR.r, misc.r, S.r], writes=[S.r])
            if mixstop == "F":
                return
            P.op("act", lambda e: e.activation(Sbf.t[:], S.t[:], AF.Copy), reads=[S.r], writes=[Sbf.r])
            P.dma("sp", artS[n * 128:(n + 1) * 128, :], a_.t[:].rearrange("p k t -> p (k t)"), reads=[a_.r], writes=[Res("x")], key=a_.r)

    def branch_phase(st, l):
        def T(name, shape, dt):
            return Tile(st.enter_context(nc.sbuf_tensor(un("br_" + name), list(shape), dt)), P.res("br_" + name))
        art = [T("art%d" % i, [128, 24, 128], BF16) for i in range(2)]
        ga = [T("ga%d" % i, [128, 512], BF16) for i in range(2)]
        gr = [T("gr%d" % i, [128, 512], BF16) for i in range(2)]
        sa = [T("sa%d" % i, [128, 512], F32) for i in range(2)]
        sr = [T("sr%d" % i, [128, 512], F32) for i in range(2)]
        tm = [T("tm%d" % i, [128, 512], F32) for i in range(2)]
        mg = [T("mg%d" % i, [128, 512], BF16) for i in range(2)]
        mst = [T("mst%d" % i, [128, 4, 128], BF16) for i in range(2)]
        k = 0
        for s in range(4):
            wa, wr = w_get(2)
            for n in range(NCH):
                i = k % 2
                k += 1
                P.dma("sp", art[i].t[:].rearrange("p k t -> p (k t)"), artS[n * 128:(n + 1) * 128, :], writes=[art[i].r])
                P.dma("sp", ga[i].t[:], uS[n * 128:(n + 1) * 128, O_GA + s * 512:O_GA + (s + 1) * 512], writes=[ga[i].r])
                P.dma("sp", gr[i].t[:], uS[n * 128:(n + 1) * 128, O_GR + s * 512:O_GR + (s + 1) * 512], writes=[gr[i].r])
                pA, pR = pf[(2 * k) % 6], pf[(2 * k + 1) % 6]
                for kc in range(8):
                    P.op("pe", lambda e: e.matmul(pA.t[:], art[i].t[:, kc, :], wa.t[:, kc, :], start=(kc == 0), stop=(kc == 7)),
                         reads=[art[i].r, wa.r], writes=[pA.r], same_ok=True)
                for kc in range(16):
                    P.op("pe", lambda e: e.matmul(pR.t[:], art[i].t[:, 8 + kc, :], wr.t[:, kc, :], start=(kc == 0), stop=(kc == 15)),
                         reads=[art[i].r, wr.r], writes=[pR.r], same_ok=True)
                P.op("act", lambda e: e.activation(sa[i].t[:], ga[i].t[:], AF.Sigmoid), reads=[ga[i].r], writes=[sa[i].r])
                P.op("act", lambda e: e.activation(sr[i].t[:], gr[i].t[:], AF.Sigmoid), reads=[gr[i].r], writes=[sr[i].r])
                P.op("dve", lambda e: e.tensor_tensor(tm[i].t[:], sa[i].t[:], pA.t[:], ALU.mult), reads=[sa[i].r, pA.r], writes=[tm[i].r])
                P.op("dve", lambda e: e.tensor_tensor(sr[i].t[:], sr[i].t[:], pR.t[:], ALU.mult), reads=[sr[i].r, pR.r], writes=[sr[i].r])
                P.op("dve", lambda e: e.tensor_tensor(mg[i].t[:], tm[i].t[:], sr[i].t[:], ALU.add), reads=[tm[i].r, sr[i].r], writes=[mg[i].r])
                transposes_to(lambda lo, hi: (mst[i].t[:, lo:hi, :], mst[i].r), lambda b: mg[i].t[:, b * 128:(b + 1) * 128], mg[i].r, 4)
                P.dma("sp", mS[n * 128:(n + 1) * 128, s * 512:(s + 1) * 512], mst[i].t[:].rearrange("p k t -> p (k t)"),
                      reads=[mst[i].r], writes=[Res("x")], key=mst[i].r)

    def resid_gemm_phase(st, l, tag, lhs_loader, nk_list, hsrc):
        def T(name, shape, dt):
            return Tile(st.enter_context(nc.sbuf_tensor(un(tag + name), list(shape), dt)), P.res("rg_" + name))
        hsl = [T("hsl%d" % i, [128, 512], F32) for i in range(2)]
        ho = [T("ho%d" % i, [128, 512], F32) for i in range(2)]
        k = 0
        for s in range(4):
            blks = w_get(len(nk_list))
            for n in range(NCH):
                i = k % 2
                k += 1
                lt = lhs_loader(n)
                P.dma("sp", hsl[i].t[:], hsrc[n * 128:(n + 1) * 128, s * 512:(s + 1) * 512], writes=[hsl[i].r])
                ps = next_pf()
                tot = sum(nk_list)
                j = 0
                for bi, nk in enumerate(nk_list):
                    for kc in range(nk):
                        P.op("pe", lambda e: e.matmul(ps.t[:], lt.t[:, j, :], blks[bi].t[:, kc, :], start=(j == 0), stop=(j == tot - 1)),
                             reads=[lt.r, blks[bi].r], writes=[ps.r], same_ok=True)
                        j += 1
                P.op("dve", lambda e: e.tensor_tensor(ho[i].t[:], ps.t[:], hsl[i].t[:], ALU.add), reads=[ps.r, hsl[i].r], writes=[ho[i].r])
                P.dma("sp", hS[n * 128:(n + 1) * 128, s * 512:(s + 1) * 512], ho[i].t[:], reads=[ho[i].r], writes=[Res("x")], key=ho[i].r)
                chk("O%d" % (k + 1))

    def outproj_phase(st, l, hsrc):
        mT = [Tile(st.enter_context(nc.sbuf_tensor(un("op_mT%d" % n), [128, 16, 128], BF16)), P.res("xT%d" % n)) for n in range(NCH)]
        for n in range(NCH):
            P.dma("sp", mT[n].t[:].rearrange("p k t -> p (k t)"), mS[n * 128:(n + 1) * 128, :], writes=[mT[n].r])
        chk("O1")
        resid_gemm_phase(st, l, "op_", lambda n: mT[n], [16], hsrc)

    def ffn_up_phase(st, l, xT):
        def T(name, shape, dt):
            return Tile(st.enter_context(nc.sbuf_tensor(un("fu_" + name), list(shape), dt)), P.res("fu_" + name))
        sgt = [T("sg%d" % i, [128, 512], F32) for i in range(2)]
        hd = [T("hd%d" % i, [128, 512], BF16) for i in range(2)]
        hst = [T("hst%d" % i, [128, 4, 128], BF16) for i in range(2)]
        k = 0
        for j in range(11):
            wg, wu = w_get(2)
            for n in range(NCH):
                i = k % 2
                k += 1
                pG, pU = pf[(2 * k) % 6], pf[(2 * k + 1) % 6]
                for kc in range(16):
                    P.op("pe", lambda e: e.matmul(pG.t[:], xT[n].t[:, kc, :], wg.t[:, kc, :], start=(kc == 0), stop=(kc == 15)),
                         reads=[xT[n].r, wg.r], writes=[pG.r], same_ok=True)
                for kc in range(16):
                    P.op("pe", lambda e: e.matmul(pU.t[:], xT[n].t[:, kc, :], wu.t[:, kc, :], start=(kc == 0), stop=(kc == 15)),
                         reads=[xT[n].r, wu.r], writes=[pU.r], same_ok=True)
                P.op("act", lambda e: e.activation(sgt[i].t[:], pG.t[:], AF.Silu), reads=[pG.r], writes=[sgt[i].r])
                P.op("dve", lambda e: e.tensor_tensor(hd[i].t[:], sgt[i].t[:], pU.t[:], ALU.mult), reads=[sgt[i].r, pU.r], writes=[hd[i].r])
                transposes_to(lambda lo, hi: (hst[i].t[:, lo:hi, :], hst[i].r), lambda b: hd[i].t[:, b * 128:(b + 1) * 128], hd[i].r, 4)
                P.dma("sp", hidS[n * 128:(n + 1) * 128, j * 512:(j + 1) * 512], hst[i].t[:].rearrange("p k t -> p (k t)"),
                      reads=[hst[i].r], writes=[Res("x")], key=hst[i].r)

    def ffn_down_phase(st, l):
        hid = [Tile(st.enter_context(nc.sbuf_tensor(un("fd_hid%d" % i), [128, 44, 128], BF16)), P.res("fd_hid%d" % i)) for i in range(2)]
        cnt = [0]

        def loader(n):
            t_ = hid[cnt[0] % 2]
            cnt[0] += 1
            P.dma("sp", t_.t[:].rearrange("p k t -> p (k t)"), hidS[n * 128:(n + 1) * 128, :], writes=[t_.r])
            return t_
        resid_gemm_phase(st, l, "fd_", loader, [16, 16, 12], hS)

    def final_phase(st):
        gain = Tile(st.enter_context(nc.sbuf_tensor(un("fn_gain"), [128, D], F32)), P.res("gain"))
        P.dma("sp", gain.t[:], gains[4], writes=[gain.r])
        hb = [Tile(st.enter_context(nc.sbuf_tensor(un("fn_hb%d" % i), [128, D], F32)), P.res("hb%d" % i)) for i in range(2)]
        yo = [Tile(st.enter_context(nc.sbuf_tensor(un("fn_yo%d" % i), [128, D], F32)), P.res("yo%d" % i)) for i in range(2)]
        junk = Tile(st.enter_context(nc.sbuf_tensor(un("fn_junk"), [128, D], BF16)), P.res("junk"))
        stat = [Tile(st.enter_context(nc.sbuf_tensor(un("fn_stat%d" % i), [128, 4], F32)), P.res("stat%d" % i)) for i in range(2)]
        for n in range(1, NCH):
            h, s_, o_ = hb[n % 2], stat[n % 2], yo[n % 2]
            P.dma("sp", h.t[:], hS[n * 128:(n + 1) * 128, :], writes=[h.r])
            P.op("act", lambda e: e.activation(junk.t[:], h.t[:], AF.Square, accum_out=s_.t[:, 0:1]), reads=[h.r], writes=[junk.r, s_.r])
            P.op("act", lambda e: e.activation(s_.t[:, 1:2], s_.t[:, 0:1], AF.Sqrt, bias=EPS, scale=1.0 / D), reads=[s_.r], writes=[s_.r])
            P.op("dve", lambda e: e.reciprocal(s_.t[:, 2:3], s_.t[:, 1:2]), reads=[s_.r], writes=[s_.r])
            P.op("dve", lambda e: e.scalar_tensor_tensor(o_.t[:], h.t[:], s_.t[:, 2:3], gain.t[:], ALU.mult, ALU.mult),
                 reads=[h.r, s_.r, gain.r], writes=[o_.r])
            P.dma("sp", y[(n - 1) * 128:n * 128, :], o_.t[:], reads=[o_.r], writes=[Res("x")], key=o_.r)
        P.wait_all("sp", [t_.r for t_ in yo])

    def _body():
        hsrc = xin
        ph = 0
        for l in range(DEPTH):
            with ExitStack() as st:
                xT = norm_phase(st, hsrc, l, "n1_")
                inproj_phase(st, l, xT)
                P.barrier()
            ph += 1
            if ph >= stop:
                return nc
            with ExitStack() as st:
                mixer_phase(st, l)
                P.barrier()
            ph += 1
            if ph >= stop:
                return nc
            with ExitStack() as st:
                branch_phase(st, l)
                P.barrier()
            ph += 1
            if ph >= stop:
                return nc
            with ExitStack() as st:
                outproj_phase(st, l, hsrc)
                P.barrier()
            ph += 1
            if ph >= stop:
                return nc
            hsrc = hS
            with ExitStack() as st:
                xT = norm_phase(st, hS, 2 + l, "n2_")
                ffn_up_phase(st, l, xT)
                P.barrier()
            ph += 1
            if ph >= stop:
                return nc
            with ExitStack() as st:
                ffn_down_phase(st, l)
                P.barrier()
            ph += 1
            if ph >= stop:
                return nc
        with ExitStack() as st:
            final_phase(st)
        return nc

    try:
        return _body()
    except StopBuild:
        P.barrier()
        return nc


def _t5_bucket(rel):
    n = np.maximum(rel, 0)
    nf = np.maximum(n, 1).astype(np.float32)
    large = 16 + (np.log(nf / np.float32(16)) / np.float32(math.log(128 / 16)) * np.float32(16)).astype(np.int32)
    large = np.minimum(large, 31)
    return np.where(n < 16, n, large)


def _prep(inputs, NOWN):
    NCH = NOWN + 1
    f32 = np.float32
    x = np.asarray(inputs["x"], f32)[0]
    meta = np.asarray(inputs["meta_tokens"], f32)
    rb = np.asarray(inputs["rel_bias"], f32)
    i = np.arange(128)
    j = np.arange(256)
    rel = i[:, None] + 128 - j[None, :]
    valid = (rel >= 0) & (rel < 128)
    bb = rb[_t5_bucket(rel)]
    bb = np.where(valid[:, :, None], bb, f32(NEGM)).transpose(0, 2, 1)
    m = np.arange(16)
    rel0 = i[:, None] - 112 - m[None, :]
    bm0 = np.where((rel0 >= 0)[:, :, None], rb[_t5_bucket(rel0)], f32(NEGM)).transpose(0, 2, 1)
    rel1 = 128 + i[:, None] - 112 - m[None, :]
    bm1 = rb[_t5_bucket(rel1)].transpose(0, 2, 1)
    bmc = np.broadcast_to(rb[31][None, :, None], (128, 16, 16))
    ld = np.log(1.0 - 2.0 ** (-5.0 - np.arange(8)))
    dect = np.zeros((128, 8, 128), f32)
    diff = i[None, :] - i[:, None]
    for h in range(8):
        dect[:, h, :] = np.where(diff >= 0, np.exp(ld[h] * np.maximum(diff, 0)), 0.0)
    qdec = np.broadcast_to(np.exp(ld[:, None] * (i[None, :] + 1.0))[None], (128, 8, 128)).astype(f32)
    kdec = np.exp(ld[None, :] * (127.0 - i[:, None])).astype(f32)
    cd = np.exp(ld * 128.0)
    angle = np.repeat((1.0 / (f32(10000.0) ** np.linspace(0.0, 1.0, 64, dtype=f32))).astype(f32), 2)
    sgn = np.tile(np.array([-1.0, 1.0]), 64)
    ksc = 128.0 ** -0.5
    g5 = np.stack([np.asarray(inputs["norm1"], f32)[0], np.asarray(inputs["norm1"], f32)[1],
                   np.asarray(inputs["norm2"], f32)[0], np.asarray(inputs["norm2"], f32)[1],
                   np.asarray(inputs["norm_f"], f32)], 0)
    gains = np.ascontiguousarray(np.broadcast_to(g5[:, None, :], (5, 128, D)))
    sinks = np.ascontiguousarray(np.broadcast_to(np.asarray(inputs["attn_sinks"], f32).reshape(1, 32), (128, 32)))
    shared = {
        "gains": gains, "sinks": sinks, "ident": np.eye(128, dtype=f32),
        "bband": np.ascontiguousarray(bb.reshape(128, 16 * 256)),
        "dect": dect.reshape(128, 1024), "qdec": np.ascontiguousarray(qdec.reshape(128, 1024)),
    }
    wflat = {}
    for nm in ("w_in", "w_attn_br", "w_ret_br", "w_out", "w_gate_up", "w_down"):
        w = np.asarray(inputs[nm], f32)
        wflat[nm] = w.reshape(w.shape[0] * w.shape[1], w.shape[2])
    in_maps = []
    for c in range(NCORES):
        xin = np.zeros((NCH * 128, D), f32)
        xin[112:128] = meta
        xin[128:] = x[c * NOWN * 128:(c + 1) * NOWN * 128]
        rot = np.zeros((NCH, 128, 512), f32)
        for n in range(NCH):
            g = 0 if n == 0 else 1 + c * NOWN + (n - 1)
            pos = (g * 128 + i - 112).astype(f32)
            theta = (pos[:, None] * angle[None, :]).astype(f32).astype(np.float64)
            cs, sn = np.cos(theta), np.sin(theta)
            rot[n, :, 0:128] = cs
            rot[n, :, 128:256] = sn * sgn[None, :]
            rot[n, :, 256:384] = cs * ksc
            rot[n, :, 384:512] = sn * sgn[None, :] * ksc
        misc = np.zeros((128, 128), f32)
        misc[:, 0:8] = kdec
        misc[:, 8:16] = (cd ** (NOWN * c))[None, :]
        for c2 in range(NCORES):
            if c2 < c:
                misc[:, 16 + 8 * c2:24 + 8 * c2] = (cd ** (NOWN * (c - 1 - c2)))[None, :]
        if c >= 1:
            misc[:, 80 + c - 1] = 1.0
        misc[:, 88] = NEGM if c == 0 else 0.0
        bmeta = np.stack([bm0, bm1 if c == 0 else bmc, bmc], 1)
        d = dict(shared)
        for nm in ("w_in", "w_attn_br", "w_ret_br", "w_out", "w_gate_up", "w_down"):
            w2 = wflat[nm]
            rws = w2.shape[0] // NCORES
            d[nm] = w2[c * rws:(c + 1) * rws]
        d.update({"xin": xin, "rot": rot.reshape(NCH * 128, 512), "misc": misc,
                  "bmeta": np.ascontiguousarray(bmeta.reshape(128, 3 * 256)).astype(f32)})
        in_maps.append(d)
    return in_maps


_NC_CACHE = {}


def run(inputs, NOWN, DEPTH=2, raw=False):
    key = (NOWN, DEPTH)
    if key not in _NC_CACHE:
        _NC_CACHE[key] = build(NOWN, DEPTH)
    nc = _NC_CACHE[key]
    in_maps = _prep(inputs, NOWN)
    res = run_bass_kernel_spmd(nc, in_maps, core_ids=list(range(NCORES)))
    if raw:
        return res.results
    out = np.concatenate([np.asarray(r["y"], np.float32) for r in res.results], axis=0)
    return out[None]


def kernel(**inputs):
    return run(inputs, 16, 2)
```

```python
import math
from contextlib import ExitStack
import numpy as np
import concourse.bass as bass
import concourse.mybir as mybir
from concourse.bass_utils import run_bass_kernel_spmd

F32 = mybir.dt.float32
BF16 = mybir.dt.bfloat16
AF = mybir.ActivationFunctionType
ALU = mybir.AluOpType
AX = mybir.AxisListType

D = 2048
DIN = 11776
DFF = 5632
NCORES = 8
O_AQ, O_AK, O_AV, O_RQ, O_RK, O_RV, O_RG, O_GA, O_GR = 0, 1024, 1280, 1536, 2560, 3584, 5632, 7680, 9728
EPS = 1e-5
NEGM = -30000.0
RING = 4


class Res:
    __slots__ = ("name", "lw", "rd", "dsem", "dcnt")

    def __init__(self, name):
        self.name = name
        self.lw = None
        self.rd = {}
        self.dsem = None
        self.dcnt = 0


class Prog:
    ENG = ("pe", "act", "dve", "pool", "sp")

    def __init__(self, nc):
        self.nc = nc
        self.e = {"pe": nc.tensor, "act": nc.scalar, "dve": nc.vector, "pool": nc.gpsimd, "sp": nc.sync}
        self.sem = {k: nc.alloc_semaphore("s_" + k) for k in self.ENG}
        self.cnt = {k: 0 for k in self.ENG}
        self.known = {k: {} for k in self.ENG}
        self.barsem = nc.alloc_semaphore("s_bar")
        self.barcnt = 0
        self.resreg = {}
        self.dkeys = []

    def res(self, name):
        r = self.resreg.get(name)
        if r is None:
            r = Res(name)
            self.resreg[name] = r
        return r

    def _wait(self, e, ev):
        if ev is None:
            return
        sem, val = ev
        k = self.known[e]
        if k.get(id(sem), 0) >= val:
            return
        self.e[e].wait_ge(sem, val)
        k[id(sem)] = val

    def _deps(self, e, reads, writes, same_ok=False):
        own = self.sem[e]
        for r in reads:
            if r.lw is not None and not (same_ok and r.lw[0] is own):
                self._wait(e, r.lw)
        for w in writes:
            if w.lw is not None and not (same_ok and w.lw[0] is own):
                self._wait(e, w.lw)
            for ev in w.rd.values():
                if not (same_ok and ev[0] is own):
                    self._wait(e, ev)

    def _commit(self, ev, reads, writes):
        for r in reads:
            r.rd[id(ev[0])] = ev
        for w in writes:
            w.lw = ev
            w.rd = {}

    def op(self, e, fn, reads=(), writes=(), same_ok=False):
        self._deps(e, reads, writes, same_ok)
        ins = fn(self.e[e])
        self.cnt[e] += 1
        ins.then_inc(self.sem[e], 1)
        ev = (self.sem[e], self.cnt[e])
        self._commit(ev, reads, writes)
        return ev

    def dma(self, q, out, in_, reads=(), writes=(), key=None):
        self._deps(q, reads, writes)
        if key is None:
            key = writes[0] if writes else reads[0]
        if key.dsem is None:
            key.dsem = self.nc.alloc_semaphore("d_" + key.name)
            self.dkeys.append(key)
        ins = self.e[q].dma_start(out=out, in_=in_)
        ins.then_inc(key.dsem, 16)
        key.dcnt += 16
        ev = (key.dsem, key.dcnt)
        self._commit(ev, reads, writes)
        return ev

    def custom(self, e, fn, sem, val, reads=(), writes=()):
        self._deps(e, reads, writes)
        ins = fn(self.e[e])
        ins.then_inc(sem, 1)
        ev = (sem, val)
        self._commit(ev, reads, writes)
        return ev

    def barrier(self):
        for o in ("pe", "act", "dve"):
            if self.cnt[o]:
                self._wait("sp", (self.sem[o], self.cnt[o]))
        for k in self.dkeys:
            if k.dcnt and k.name[:2] != "w_":
                self._wait("sp", (k.dsem, k.dcnt))
        self.e["sp"].sem_inc(self.barsem, 1)
        self.barcnt += 1
        for o in ("pe", "act", "dve"):
            self.e[o].wait_ge(self.barsem, self.barcnt)
            kn = self.known[o]
            for o2 in ("pe", "act", "dve"):
                kn[id(self.sem[o2])] = self.cnt[o2]
            for k in self.dkeys:
                if k.name[:2] != "w_":
                    kn[id(k.dsem)] = k.dcnt

    def wait_all(self, e, ress):
        for r in ress:
            self._wait(e, r.lw)
            for ev in r.rd.values():
                self._wait(e, ev)


class StopBuild(Exception):
    pass


class Tile:
    __slots__ = ("t", "r")

    def __init__(self, t, r):
        self.t = t
        self.r = r


def build(NOWN, DEPTH, stop=99, debug=False, mixstop=None):
    NCH = NOWN + 1
    nc = bass.Bass("TRN2", target_bir_lowering=False)
    P = Prog(nc)

    def din(name, shape, dt=F32):
        return nc.dram_tensor(name, list(shape), dt, kind="ExternalInput").ap()

    xin = din("xin", [NCH * 128, D])
    wspec = (("w_in", D, DIN), ("w_attn_br", 1024, D), ("w_ret_br", D, D), ("w_out", D, D),
             ("w_gate_up", D, 2 * DFF), ("w_down", DFF, D))
    wsh_in = {nm: din(nm, [2 * K_ // NCORES, N_]) for (nm, K_, N_) in wspec}
    wsh_bf = {nm: nc.dram_tensor(nm + "_shb", [2 * K_ // NCORES, N_], BF16).ap() for (nm, K_, N_) in wspec}
    wfull = {nm: nc.dram_tensor(nm + "_full", [2 * K_, N_], BF16).ap() for (nm, K_, N_) in wspec}
    gains = din("gains", [5, 128, D])
    sinks = din("sinks", [128, 32])
    ident_d = din("ident", [128, 128])
    bband_d = din("bband", [128, 16 * 256])
    bmeta_d = din("bmeta", [128, 3 * 256])
    rot_d = din("rot", [NCH * 128, 512])
    dect_d = din("dect", [128, 1024])
    qdec_d = din("qdec", [128, 1024])
    misc_d = din("misc", [128, 128])
    y = nc.dram_tensor("y", [NOWN * 128, D], F32, kind="ExternalOutput").ap()

    skind = "ExternalOutput" if debug else "Internal"
    uS = nc.dram_tensor("uS", [NCH * 128, DIN], BF16, kind=skind).ap()
    hS = nc.dram_tensor("hS", [NCH * 128, D], F32, kind=skind).ap()
    artS = nc.dram_tensor("artS", [NCH * 128, 24 * 128], BF16, kind=skind).ap()
    mS = nc.dram_tensor("mS", [NCH * 128, 16 * 128], BF16, kind=skind).ap()
    hidS = nc.dram_tensor("hidS", [NCH * 128, 44 * 128], BF16, kind=skind).ap()
    cinL = nc.dram_tensor("cinL", [128, 2048], F32).ap()
    coutL = nc.dram_tensor("coutL", [NCORES * 128, 2048], F32).ap()
    cinK = nc.dram_tensor("cinK", [128, 512], BF16).ap()
    coutK = nc.dram_tensor("coutK", [NCORES * 128, 512], BF16).ap()

    ccsem = nc.alloc_semaphore("ccsem")
    cccnt = [0]
    ucnt = [0]

    def un(name):
        ucnt[0] += 1
        return "%s_%d" % (name, ucnt[0])

    def chk(name):
        if mixstop == name:
            raise StopBuild()

    def ptile(name, shape, dt):
        return Tile(nc.alloc_sbuf_tensor(name, list(shape), dt), P.res(name))

    identb = ptile("identb", [128, 128], BF16)
    ring = [ptile("w_ring%d" % i, [128, 16, 512], BF16) for i in range(RING)]
    pf = [Tile(nc.alloc_psum_tensor("pf%d" % i, [128, 512], F32), P.res("pf%d" % i)) for i in range(6)]
    pb = [Tile(nc.alloc_psum_tensor("pb%d" % i, [128, 1024], BF16), P.res("pb%d" % i)) for i in range(2)]

    P.dma("pool", identb.t[:], ident_d, writes=[identb.r])
    wres = {}
    for (nm, K_, N_) in wspec:
        rs_, rf_ = P.res("wsh_" + nm), P.res("w_full_" + nm)
        P.dma("pool", wsh_bf[nm], wsh_in[nm], writes=[rs_], key=rs_)
        wres[nm] = rf_
    for (nm, K_, N_) in wspec:
        cccnt[0] += 1
        P.custom("pool", lambda e: e.collective_compute("AllGather", ALU.bypass, replica_groups=[list(range(NCORES))],
                                                        ins=[wsh_bf[nm]], outs=[wfull[nm]]),
                 ccsem, cccnt[0], reads=[P.res("wsh_" + nm)], writes=[wres[nm]])
    w_in = wfull["w_in"].rearrange("(l k) n -> l k n", l=2)
    w_ab = wfull["w_attn_br"].rearrange("(l k) n -> l k n", l=2)
    w_rb = wfull["w_ret_br"].rearrange("(l k) n -> l k n", l=2)
    w_out = wfull["w_out"].rearrange("(l k) n -> l k n", l=2)
    w_gu = wfull["w_gate_up"].rearrange("(l k) n -> l k n", l=2)
    w_dn = wfull["w_down"].rearrange("(l k) n -> l k n", l=2)

    wlist = []
    for l in range(DEPTH):
        for s in range(23):
            wlist.append((w_in[l, :, s * 512:(s + 1) * 512], 16, "w_in"))
        for s in range(4):
            wlist.append((w_ab[l, :, s * 512:(s + 1) * 512], 8, "w_attn_br"))
            wlist.append((w_rb[l, :, s * 512:(s + 1) * 512], 16, "w_ret_br"))
        for s in range(4):
            wlist.append((w_out[l, :, s * 512:(s + 1) * 512], 16, "w_out"))
        for j in range(11):
            wlist.append((w_gu[l, :, j * 512:(j + 1) * 512], 16, "w_gate_up"))
            wlist.append((w_gu[l, :, DFF + j * 512:DFF + (j + 1) * 512], 16, "w_gate_up"))
        for s in range(4):
            for (k0, kc) in ((0, 16), (16, 16), (32, 12)):
                wlist.append((w_dn[l, k0 * 128:(k0 + kc) * 128, s * 512:(s + 1) * 512], kc, "w_down"))
    wstate = {"issued": 0, "used": 0}

    def w_issue_upto(i):
        while wstate["issued"] <= min(i, len(wlist) - 1):
            j = wstate["issued"]
            src, kc, wn = wlist[j]
            blk = ring[j % RING]
            P.dma("pool", blk.t[:, 0:kc, :], src.rearrange("(kc p) n -> p kc n", p=128), reads=[wres[wn]], writes=[blk.r])
            wstate["issued"] += 1

    def w_get(count):
        i = wstate["used"]
        w_issue_upto(i + RING - 1)
        wstate["used"] += count
        return [ring[(i + j) % RING] for j in range(count)]

    def w_next():
        return w_get(1)[0]

    pfrot = [0]

    def next_pf(k=6):
        i = pfrot[0] % k
        pfrot[0] += 1
        return pf[i]

    evrot = [0]
    evmode = ["alt"]

    def evac_copy(out_ap, out_r, in_ap, in_r):
        evrot[0] += 1
        if evmode[0] == "act" or evrot[0] % 2:
            P.op("act", lambda e: e.activation(out_ap, in_ap, AF.Copy), reads=[in_r], writes=[out_r])
        else:
            P.op("dve", lambda e: e.tensor_copy(out_ap, in_ap), reads=[in_r], writes=[out_r])

    def transposes_to(dst_fn, src_ap_fn, src_r, nblk, dst_r_list=None):
        for b0 in range(0, nblk, 8):
            nb = min(8, nblk - b0)
            bank = pb[(b0 // 8) % 2]
            for i in range(nb):
                P.op("pe", lambda e: e.transpose(bank.t[:, i * 128:(i + 1) * 128], src_ap_fn(b0 + i), identb.t[:]),
                     reads=[src_r, identb.r], writes=[bank.r], same_ok=True)
            dst_ap, dst_r = dst_fn(b0, b0 + nb)
            evac_copy(dst_ap, dst_r, bank.t[:, 0:nb * 128].rearrange("p (k t) -> p k t", t=128), bank.r)

    def norm_phase(st, src, gain_idx, prefix):
        gain = Tile(st.enter_context(nc.sbuf_tensor(un(prefix + "gain"), [128, D], F32)), P.res("gain"))
        P.dma("sp", gain.t[:], gains[gain_idx], writes=[gain.r])
        hb = [Tile(st.enter_context(nc.sbuf_tensor(un(prefix + "hb%d" % i), [128, D], F32)), P.res("hb%d" % i)) for i in range(2)]
        xn = [Tile(st.enter_context(nc.sbuf_tensor(un(prefix + "xn%d" % i), [128, D], BF16)), P.res("xn%d" % i)) for i in range(2)]
        junk = Tile(st.enter_context(nc.sbuf_tensor(un(prefix + "junk"), [128, D], BF16)), P.res("junk"))
        stat = [Tile(st.enter_context(nc.sbuf_tensor(un(prefix + "stat%d" % i), [128, 4], F32)), P.res("stat%d" % i)) for i in range(2)]
        xT = [Tile(st.enter_context(nc.sbuf_tensor(un(prefix + "xT%d" % n), [128, 16, 128], BF16)), P.res("xT%d" % n)) for n in range(NCH)]
        for n in range(NCH):
            h = hb[n % 2]
            P.dma("sp", h.t[:], src[n * 128:(n + 1) * 128, :], writes=[h.r])
            s_ = stat[n % 2]
            x_ = xn[n % 2]
            P.op("act", lambda e: e.activation(junk.t[:], h.t[:], AF.Square, accum_out=s_.t[:, 0:1]),
                 reads=[h.r], writes=[junk.r, s_.r])
            P.op("act", lambda e: e.activation(s_.t[:, 1:2], s_.t[:, 0:1], AF.Sqrt, bias=EPS, scale=1.0 / D),
                 reads=[s_.r], writes=[s_.r])
            P.op("dve", lambda e: e.reciprocal(s_.t[:, 2:3], s_.t[:, 1:2]), reads=[s_.r], writes=[s_.r])
            P.op("dve", lambda e: e.scalar_tensor_tensor(x_.t[:], h.t[:], s_.t[:, 2:3], gain.t[:], ALU.mult, ALU.mult),
                 reads=[h.r, s_.r, gain.r], writes=[x_.r])
            transposes_to(lambda lo, hi: (xT[n].t[:, lo:hi, :], xT[n].r),
                          lambda b: x_.t[:, b * 128:(b + 1) * 128], x_.r, 16)
        return xT

    def inproj_phase(st, l, xT):
        ust = [Tile(st.enter_context(nc.sbuf_tensor(un("ip_ust%d" % i), [128, 512], BF16)), P.res("st%d" % i)) for i in range(4)]
        k = 0
        for s in range(23):
            blk = w_next()
            for n in range(NCH):
                ps = next_pf()
                for kc in range(16):
                    P.op("pe", lambda e: e.matmul(ps.t[:], xT[n].t[:, kc, :], blk.t[:, kc, :], start=(kc == 0), stop=(kc == 15)),
                         reads=[xT[n].r, blk.r], writes=[ps.r], same_ok=True)
                u = ust[k % 4]
                k += 1
                evac_copy(u.t[:], u.r, ps.t[:], ps.r)
                P.dma("sp", uS[n * 128:(n + 1) * 128, s * 512:(s + 1) * 512], u.t[:], reads=[u.r], writes=[Res("x")], key=u.r)

    def mixer_phase(st, l):
        def T(name, shape, dt, rname=None):
            return Tile(st.enter_context(nc.sbuf_tensor(un("mx_" + name), list(shape), dt)), P.res(rname or ("mx_" + name)))

        if mixstop == "Z":
            return
        bband = T("bband", [128, 16, 256], F32)
        bmeta = T("bmeta", [128, 3, 16, 16], F32)
        dect = T("dect", [128, 8, 128], F32)
        qdec = T("qdec", [128, 8, 128], F32)
        misc = T("misc", [128, 128], F32)
        sinkb = T("sinkb", [128, 32], F32)
        cres = P.res("mx_consts")
        for (t_, src) in ((bband, bband_d), (bmeta, bmeta_d), (dect, dect_d), (qdec, qdec_d), (misc, misc_d), (sinkb, sinks)):
            P.dma("sp", t_.t[:].rearrange("p a b -> p (a b)") if len(t_.t.shape) == 3 else
                  (t_.t[:].rearrange("p a b c -> p (a b c)") if len(t_.t.shape) == 4 else t_.t[:]), src, writes=[t_.r], key=cres)
        for t_ in (bband, bmeta, dect, qdec, misc, sinkb):
            t_.r.lw = (cres.dsem, cres.dcnt)
        if mixstop == "Y":
            return
        kdec = misc.t[:, 0:8]
        cmask = misc.t[:, 88:89]
        cdh = [math.exp(math.log(1.0 - 2.0 ** (-5 - h)) * 128.0) for h in range(8)]

        uA = T("uA", [128, 1536], BF16)
        uR = T("uR", [128, 6144], BF16)
        rotb = [T("rot%d" % i, [128, 512], F32) for i in range(2)]
        S = T("S", [128, 8, 256], F32)
        Sbf = T("Sbf", [128, 8, 256], BF16)
        t1 = T("t1", [128, 8, 128], F32)
        t2 = T("t2", [128, 8, 128], F32)
        qr = T("qr", [128, 8, 128], BF16)
        kr = T("kr", [128, 8, 128], BF16)
        kw = T("kw", [128, 8, 128], BF16)
        qT = T("qT", [128, 8, 128], BF16)
        qwT = T("qwT", [128, 8, 128], BF16)
        kT = T("kT", [128, 8, 128], BF16)
        sTs = [T("sTs%d" % i, [128, 4, 128], BF16) for i in range(2)]
        sg1 = T("sg", [128, 1024], F32)
        sg = [sg1, sg1]
        orr = [T("orr%d" % i, [128, 1024], BF16) for i in range(2)]
        rst = T("rst", [128, 32], F32)
        aqT = T("aqT", [64, 16, 128], BF16)
        kTd = [T("kTd%d" % i, [128, 4, 128], BF16) for i in range(2)]
        kTd0 = T("kTd0", [128, 4, 128], BF16)
        kTdh = T("kTdh", [128, 4, 128], BF16)
        vk = [T("vk%d" % i, [128, 256], BF16) for i in range(2)]
        vkh = T("vkh", [128, 256], BF16)
        vmeta = T("vmeta", [16, 256], BF16)
        sb = [T("sb%d" % i, [128, 4, 272], F32) for i in range(2)]
        pp = [T("pp%d" % i, [128, 4, 272], BF16) for i in range(2)]
        pT = [T("pT%d" % i, [128, 8, 128], BF16) for i in range(2)]
        pTm = [T("pTm%d" % i, [16, 4, 128], BF16) for i in range(2)]
        ast = T("ast", [128, 64], F32)
        oa = T("oa", [128, 16, 64], BF16)
        art1 = T("art", [128, 24, 128], BF16)
        art = [art1, art1]
        hk = [T("hk%d" % i, [128, 512], BF16) for i in range(2)]
        hacc = T("hacc", [128, 512], F32)
        hkv = T("hkv", [128, 512], BF16)

        rcinK, rcoutK, rcinL, rcoutL = P.res("cinK"), P.res("coutK"), P.res("cinL"), P.res("coutL")
        if mixstop != "A2":
            P.dma("sp", hkv.t[:], uS[(NCH - 1) * 128:NCH * 128, O_AK:O_AK + 512], writes=[hkv.r])
            P.dma("pool", cinK, hkv.t[:], reads=[hkv.r], writes=[rcinK], key=rcinK)
        if mixstop == "A1":
            return
        P.dma("sp", vmeta.t[:], uS[112:128, O_AV:O_AV + 256], writes=[vmeta.r])
        if mixstop in ("A", "A2"):
            return

        def rotary(src_ap3, cos_ap, ss_ap, dst):
            s4 = src_ap3.rearrange("p h (d two) -> p h d two", two=2)
            ss3 = ss_ap.rearrange("p (d two) -> p d two", two=2)
            t24 = t2.t[:].rearrange("p h (d two) -> p h d two", two=2)
            rd = dst["reads"]
            P.op("dve", lambda e: e.tensor_tensor(t1.t[:], src_ap3, cos_ap.unsqueeze(1).broadcast_to([128, 8, 128]), ALU.mult),
                 reads=rd, writes=[t1.r])
            P.op("dve", lambda e: e.tensor_tensor(t24[:, :, :, 0], s4[:, :, :, 1], ss3[:, :, 0].unsqueeze(1).broadcast_to([128, 8, 64]), ALU.mult),
                 reads=rd, writes=[t2.r])
            P.op("dve", lambda e: e.tensor_tensor(t24[:, :, :, 1], s4[:, :, :, 0], ss3[:, :, 1].unsqueeze(1).broadcast_to([128, 8, 64]), ALU.mult),
                 reads=rd, writes=[t2.r], same_ok=True)
            P.op("dve", lambda e: e.tensor_tensor(dst["t"].t[:], t1.t[:], t2.t[:], ALU.add), reads=[t1.r, t2.r], writes=[dst["t"].r])

        def state_update(Stile, v3, vr, halves=(0, 1)):
            for r in halves:
                banks = (pf[3], pf[4])
                for hh in range(4):
                    h = 4 * r + hh
                    bk = banks[hh // 2]
                    P.op("pe", lambda e: e.matmul(bk.t[:, (hh % 2) * 256:(hh % 2 + 1) * 256], kw.t[:, h, :], v3[:, h, :], start=True, stop=True),
                         reads=[kw.r, vr], writes=[bk.r], same_ok=True)
                for hh in range(4):
                    h = 4 * r + hh
                    bk = banks[hh // 2]
                    P.op("dve", lambda e: e.scalar_tensor_tensor(Stile.t[:, h, :], Stile.t[:, h, :], float(cdh[h]),
                                                                 bk.t[:, (hh % 2) * 256:(hh % 2 + 1) * 256], ALU.mult, ALU.add),
                         reads=[bk.r], writes=[Stile.r])

        def load_rot(n):
            rb = rotb[n % 2]
            P.dma("sp", rb.t[:], rot_d[n * 128:(n + 1) * 128, :], writes=[rb.r])
            return rb

        P.op("dve", lambda e: e.memset(S.t[:], 0.0), writes=[S.r])
        for n in range(1, NCH):
            u2 = uR
            P.dma("sp", u2.t[:, 1024:4096], uS[n * 128:(n + 1) * 128, O_RK:O_RK + 3072], writes=[u2.r])
            rb = load_rot(n)
            k3 = u2.t[:, 1024:2048].rearrange("p (h d) -> p h d", h=8)
            v3 = u2.t[:, 2048:4096].rearrange("p (h e) -> p h e", h=8)
            rotary(k3, rb.t[:, 256:384], rb.t[:, 384:512], {"t": kr, "reads": [u2.r, rb.r]})
            P.op("dve", lambda e: e.tensor_tensor(kw.t[:], kr.t[:], kdec.unsqueeze(2).broadcast_to([128, 8, 128]), ALU.mult),
                 reads=[kr.r, misc.r], writes=[kw.r])
            state_update(S, v3, u2.r)
        P.dma("pool", cinL, S.t[:].rearrange("p h e -> p (h e)"), reads=[S.r], writes=[rcinL], key=rcinL)
        if mixstop == "B":
            return
        for (ci, co, rci, rco) in ((cinK, coutK, rcinK, rcoutK), (cinL, coutL, rcinL, rcoutL)):
            cccnt[0] += 1
            P.custom("pool", lambda e: e.collective_compute("AllGather", ALU.bypass, replica_groups=[list(range(NCORES))],
                                                            ins=[ci], outs=[co]),
                     ccsem, cccnt[0], reads=[rci], writes=[rco])

        if mixstop == "C":
            P.wait_all("sp", [rcoutK, rcoutL])
            return
        def make_kTd(k_ap, k_r, dst):
            bank = pb[1]
            for g in range(4):
                P.op("pe", lambda e: e.transpose(bank.t[0:64, g * 128:(g + 1) * 128], k_ap[:, g * 64:(g + 1) * 64], identb.t[:]),
                     reads=[k_r, identb.r], writes=[bank.r], same_ok=True)
            evac_copy(dst.t[0:64, :, :], dst.r, bank.t[0:64, 0:512].rearrange("p (g t) -> p g t", t=128), bank.r)

        def attention(n, u, kprev, vprev_ap, vprev_r, kcur, vcur_ap, vcur_r):
            mv = 0 if n == 0 else (1 if n == 1 else 2)
            for i2 in range(2):
                bank = pb[i2]
                for hq in range(8):
                    h = 8 * i2 + hq
                    P.op("pe", lambda e: e.transpose(bank.t[0:64, hq * 128:(hq + 1) * 128], u.t[:, O_AQ + h * 64:O_AQ + (h + 1) * 64], identb.t[:]),
                         reads=[u.r, identb.r], writes=[bank.r], same_ok=True)
                evac_copy(aqT.t[:, 8 * i2:8 * i2 + 8, :], aqT.r, bank.t[0:64, :].rearrange("p (k t) -> p k t", t=128), bank.r)
            chk("D1")
            pfm = pf[2]
            for g in range(4):
                bX, bY = pf[0], pf[1]
                s_ = sb[g % 2]
                p_ = pp[g % 2]
                for hh in range(4):
                    h = 4 * g + hh
                    bk = bX if hh < 2 else bY
                    o0 = (hh % 2) * 256
                    P.op("pe", lambda e: e.matmul(bk.t[:, o0:o0 + 128], aqT.t[:, h, :], kprev.t[0:64, g, :], start=True, stop=True),
                         reads=[aqT.r, kprev.r], writes=[bk.r], same_ok=True)
                    P.op("pe", lambda e: e.matmul(bk.t[:, o0 + 128:o0 + 256], aqT.t[:, h, :], kcur.t[0:64, g, :], start=True, stop=True),
                         reads=[aqT.r, kcur.r], writes=[bk.r], same_ok=True)
                    P.op("pe", lambda e: e.matmul(pfm.t[:, h * 16:(h + 1) * 16], aqT.t[:, h, :], kTd0.t[0:64, g, 112:128], start=True, stop=True),
                         reads=[aqT.r, kTd0.r], writes=[pfm.r], same_ok=True)
                for i2, bk in enumerate((bX, bY)):
                    P.op("dve", lambda e: e.scalar_tensor_tensor(s_.t[:, 2 * i2:2 * i2 + 2, 0:256], bk.t[:].rearrange("p (a j) -> p a j", a=2), 0.125,
                                                                 bband.t[:, 4 * g + 2 * i2:4 * g + 2 * i2 + 2, :], ALU.mult, ALU.add),
                         reads=[bk.r, bband.r], writes=[s_.r], same_ok=(i2 == 1))
                P.op("dve", lambda e: e.scalar_tensor_tensor(s_.t[:, :, 256:272], pfm.t[:, 64 * g:64 * g + 64].rearrange("p (a m) -> p a m", a=4), 0.125,
                                                             bmeta.t[:, mv, 4 * g:4 * g + 4, :], ALU.mult, ALU.add),
                     reads=[pfm.r, bmeta.r], writes=[s_.r], same_ok=True)
                if n == 0:
                    P.op("dve", lambda e: e.tensor_scalar(s_.t[:, :, 0:256], s_.t[:, :, 0:256], NEGM, None, ALU.add), reads=[s_.r], writes=[s_.r])
                elif n == 1:
                    P.op("dve", lambda e: e.tensor_scalar(s_.t[:, :, 0:128], s_.t[:, :, 0:128], cmask, None, ALU.add), reads=[s_.r, misc.r], writes=[s_.r])
                chk("D2")
                P.op("dve", lambda e: e.tensor_reduce(ast.t[:, 4 * g:4 * g + 4], s_.t[:], AX.X, ALU.max), reads=[s_.r], writes=[ast.r])
                P.op("dve", lambda e: e.tensor_tensor(ast.t[:, 4 * g:4 * g + 4], ast.t[:, 4 * g:4 * g + 4], sinkb.t[:, 16 * l + 4 * g:16 * l + 4 * g + 4], ALU.max),
                     reads=[ast.r, sinkb.r], writes=[ast.r])
                P.op("dve", lambda e: e.tensor_scalar(ast.t[:, 16 + 4 * g:20 + 4 * g], ast.t[:, 4 * g:4 * g + 4], -1.0, None, ALU.mult),
                     reads=[ast.r], writes=[ast.r])
                for hh in range(4):
                    h = 4 * g + hh
                    P.op("act", lambda e: e.activation(p_.t[:, hh, :], s_.t[:, hh, :], AF.Exp, bias=ast.t[:, 16 + h:17 + h], scale=1.0,
                                                       accum_out=ast.t[:, 32 + h:33 + h]),
                         reads=[s_.r, ast.r], writes=[p_.r, ast.r], same_ok=(hh > 0))
                chk("D3")
                b0, b1 = pb[0], pb[1]
                for hh in range(4):
                    P.op("pe", lambda e: e.transpose(b0.t[:, hh * 128:(hh + 1) * 128], p_.t[:, hh, 0:128], identb.t[:]),
                         reads=[p_.r, identb.r], writes=[b0.r], same_ok=True)
                    P.op("pe", lambda e: e.transpose(b0.t[:, (4 + hh) * 128:(5 + hh) * 128], p_.t[:, hh, 128:256], identb.t[:]),
                         reads=[p_.r, identb.r], writes=[b0.r], same_ok=True)
                    P.op("pe", lambda e: e.transpose(b1.t[0:16, hh * 128:(hh + 1) * 128], p_.t[:, hh, 256:272], identb.t[:]),
                         reads=[p_.r, identb.r], writes=[b1.r], same_ok=True)
                pt_, ptm_ = pT[g % 2], pTm[g % 2]
                evac_copy(pt_.t[:], pt_.r, b0.t[:].rearrange("p (k t) -> p k t", t=128), b0.r)
                evac_copy(ptm_.t[:], ptm_.r, b1.t[0:16, 0:512].rearrange("p (k t) -> p k t", t=128), b1.r)
                chk("D4")
                for hh in range(4):
                    h = 4 * g + hh
                    ob = pf[3] if h < 8 else pf[4]
                    oo = (h % 8) * 64
                    P.op("pe", lambda e: e.matmul(ob.t[:, oo:oo + 64], pt_.t[:, hh, :], vprev_ap[:, g * 64:(g + 1) * 64], start=True, stop=False),
                         reads=[pt_.r, vprev_r], writes=[ob.r], same_ok=True)
                    P.op("pe", lambda e: e.matmul(ob.t[:, oo:oo + 64], pt_.t[:, 4 + hh, :], vcur_ap[:, g * 64:(g + 1) * 64], start=False, stop=False),
                         reads=[pt_.r, vcur_r], writes=[ob.r], same_ok=True)
                    P.op("pe", lambda e: e.matmul(ob.t[:, oo:oo + 64], ptm_.t[:, hh, :], vmeta.t[:, g * 64:(g + 1) * 64], start=False, stop=True),
                         reads=[ptm_.r, vmeta.r], writes=[ob.r], same_ok=True)
            chk("D5")
            P.op("dve", lambda e: e.tensor_tensor(ast.t[:, 48:64], sinkb.t[:, 16 * l:16 * l + 16], ast.t[:, 16:32], ALU.add),
                 reads=[ast.r, sinkb.r], writes=[ast.r])
            P.op("act", lambda e: e.activation(ast.t[:, 48:64], ast.t[:, 48:64], AF.Exp), reads=[ast.r], writes=[ast.r])
            P.op("dve", lambda e: e.tensor_tensor(ast.t[:, 48:64], ast.t[:, 48:64], ast.t[:, 32:48], ALU.add), reads=[ast.r], writes=[ast.r])
            P.op("dve", lambda e: e.reciprocal(ast.t[:, 48:64], ast.t[:, 48:64]), reads=[ast.r], writes=[ast.r])
            for i2, ob in enumerate((pf[3], pf[4])):
                P.op("dve", lambda e: e.tensor_tensor(oa.t[:, 8 * i2:8 * i2 + 8, :], ob.t[:].rearrange("p (h d) -> p h d", h=8),
                                                      ast.t[:, 48 + 8 * i2:56 + 8 * i2].unsqueeze(2).broadcast_to([128, 8, 64]), ALU.mult),
                     reads=[ob.r, ast.r], writes=[oa.r], same_ok=(i2 == 1))

        P.op("dve", lambda e: e.memset(S.t[:], 0.0), reads=[], writes=[S.r])
        P.op("dve", lambda e: e.memset(Sbf.t[:], 0.0), writes=[Sbf.r])
        for n in range(NCH):
            u = uA
            P.dma("sp", uA.t[:], uS[n * 128:(n + 1) * 128, 0:1536], writes=[uA.r])
            P.dma("sp", uR.t[:], uS[n * 128:(n + 1) * 128, 1536:7680], writes=[uR.r])
            rb = load_rot(n)
            a_ = art[n % 2]
            kc_ = kTd0 if n == 0 else kTd[n % 2]
            make_kTd(u.t[:, O_AK:O_AK + 256], u.r, kc_)
            vc_ = vk[n % 2]
            P.op("act", lambda e: e.activation(vc_.t[:], u.t[:, O_AV:O_AV + 256], AF.Copy), reads=[u.r], writes=[vc_.r])
            if n == 0:
                kp_, vp_ = kc_, vc_
            elif n == 1:
                for r_ in range(NCORES - 1):
                    hk_ = hk[r_ % 2]
                    P.dma("sp", hk_.t[:], coutK[r_ * 128:(r_ + 1) * 128, :], reads=[rcoutK], writes=[hk_.r])
                    if r_ == 0:
                        P.op("dve", lambda e: e.tensor_scalar(hacc.t[:], hk_.t[:], misc.t[:, 80:81], None, ALU.mult),
                             reads=[hk_.r, misc.r], writes=[hacc.r])
                    else:
                        P.op("dve", lambda e: e.scalar_tensor_tensor(hacc.t[:], hk_.t[:], misc.t[:, 80 + r_:81 + r_], hacc.t[:], ALU.mult, ALU.add),
                             reads=[hk_.r, misc.r], writes=[hacc.r])
                P.op("dve", lambda e: e.tensor_copy(hkv.t[:], hacc.t[:]), reads=[hacc.r], writes=[hkv.r])
                make_kTd(hkv.t[:, 0:256], hkv.r, kTdh)
                P.op("act", lambda e: e.activation(vkh.t[:], hkv.t[:, 256:512], AF.Copy), reads=[hkv.r], writes=[vkh.r])
                kp_, vp_ = kTdh, vkh
            else:
                kp_, vp_ = (kTd0 if n - 1 == 0 else kTd[(n - 1) % 2]), vk[(n - 1) % 2]
            attention(n, u, kp_, vp_.t[:], vp_.r, kc_, vc_.t[:], vc_.r)
            transposes_to(lambda lo, hi: (a_.t[:, lo:hi, :], a_.r),
                          lambda b: oa.t[:, 2 * b:2 * b + 2, :].rearrange("p a d -> p (a d)"), oa.r, 8)
            if mixstop == "D":
                return
            u = uR
            q3 = u.t[:, 0:1024].rearrange("p (h d) -> p h d", h=8)
            k3 = u.t[:, 1024:2048].rearrange("p (h d) -> p h d", h=8)
            v3 = u.t[:, 2048:4096].rearrange("p (h e) -> p h e", h=8)
            rotary(q3, rb.t[:, 0:128], rb.t[:, 128:256], {"t": qr, "reads": [u.r, rb.r]})
            rotary(k3, rb.t[:, 256:384], rb.t[:, 384:512], {"t": kr, "reads": [u.r, rb.r]})
            P.op("dve", lambda e: e.tensor_tensor(kw.t[:], kr.t[:], kdec.unsqueeze(2).broadcast_to([128, 8, 128]), ALU.mult),
                 reads=[kr.r, misc.r], writes=[kw.r])
            for i in range(8):
                P.op("pe", lambda e: e.transpose(pb[0].t[:, i * 128:(i + 1) * 128], qr.t[:, i, :], identb.t[:]),
                     reads=[qr.r, identb.r], writes=[pb[0].r], same_ok=True)
            for i in range(8):
                P.op("pe", lambda e: e.transpose(pb[1].t[:, i * 128:(i + 1) * 128], kr.t[:, i, :], identb.t[:]),
                     reads=[kr.r, identb.r], writes=[pb[1].r], same_ok=True)
            pb0v = pb[0].t[:].rearrange("p (k t) -> p k t", t=128)
            P.op("act", lambda e: e.activation(qT.t[:], pb0v, AF.Copy), reads=[pb[0].r], writes=[qT.r])
            P.op("dve", lambda e: e.tensor_tensor(qwT.t[:], pb0v, qdec.t[:], ALU.mult), reads=[pb[0].r, qdec.r], writes=[qwT.r])
            evac_copy(kT.t[:], kT.r, pb[1].t[:].rearrange("p (k t) -> p k t", t=128), pb[1].r)
            for r in range(2):
                sT_ = sTs[r]
                for hh in range(4):
                    h = 4 * r + hh
                    P.op("pe", lambda e: e.matmul(pf[2].t[:, hh * 128:(hh + 1) * 128], kT.t[:, h, :], qT.t[:, h, :], start=True, stop=True),
                         reads=[kT.r, qT.r], writes=[pf[2].r], same_ok=True)
                P.op("dve", lambda e: e.tensor_tensor(sT_.t[:], pf[2].t[:].rearrange("p (a i) -> p a i", a=4), dect.t[:, 4 * r:4 * r + 4, :], ALU.mult),
                     reads=[pf[2].r, dect.r], writes=[sT_.r])
                banks = (pf[0], pf[1])
                for hh in range(4):
                    h = 4 * r + hh
                    bk = banks[hh // 2]
                    o0 = (hh % 2) * 256
                    P.op("pe", lambda e: e.matmul(bk.t[:, o0:o0 + 256], sT_.t[:, hh, :], v3[:, h, :], start=True, stop=False),
                         reads=[sT_.r, u.r], writes=[bk.r], same_ok=True)
                    P.op("pe", lambda e: e.matmul(bk.t[:, o0:o0 + 256], qwT.t[:, h, :], Sbf.t[:, h, :], start=False, stop=True),
                         reads=[qwT.r, Sbf.r], writes=[bk.r], same_ok=True)
                for hh in range(4):
                    h = 4 * r + hh
                    bk = banks[hh // 2]
                    o0 = (hh % 2) * 256
                    P.op("act", lambda e: e.activation(t1.t[:, 0:2, :].rearrange("p a d -> p (a d)"), bk.t[:, o0:o0 + 256], AF.Square,
                                                       accum_out=rst.t[:, h:h + 1]),
                         reads=[bk.r], writes=[t1.r, rst.r], same_ok=(hh > 0))
                P.op("act", lambda e: e.activation(rst.t[:, 8 + 4 * r:12 + 4 * r], rst.t[:, 4 * r:4 * r + 4], AF.Sqrt, bias=EPS, scale=1.0 / 256),
                     reads=[rst.r], writes=[rst.r])
                P.op("dve", lambda e: e.reciprocal(rst.t[:, 16 + 4 * r:20 + 4 * r], rst.t[:, 8 + 4 * r:12 + 4 * r]), reads=[rst.r], writes=[rst.r])
                g_ = sg[r]
                P.op("act", lambda e: e.activation(g_.t[:], u.t[:, 4096 + 1024 * r:4096 + 1024 * (r + 1)], AF.Silu), reads=[u.r], writes=[g_.r])
                o_ = orr[r]
                for hh in range(4):
                    h = 4 * r + hh
                    bk = banks[hh // 2]
                    o0 = (hh % 2) * 256
                    P.op("dve", lambda e: e.scalar_tensor_tensor(o_.t[:, hh * 256:(hh + 1) * 256], bk.t[:, o0:o0 + 256], rst.t[:, 16 + h:17 + h],
                                                                 g_.t[:, hh * 256:(hh + 1) * 256], ALU.mult, ALU.mult),
                         reads=[bk.r, rst.r, g_.r], writes=[o_.r], same_ok=(hh > 0))
                transposes_to(lambda lo, hi: (a_.t[:, 8 + 8 * r + lo:8 + 8 * r + hi, :], a_.r),
                              lambda b: o_.t[:, b * 128:(b + 1) * 128], o_.r, 8)
            if mixstop == "E":
                return
            state_update(S, v3, u.r)
            if n == 0:
                for h in range(8):
                    P.op("dve", lambda e: e.tensor_scalar(S.t[:, h, :], S.t[:, h, :], misc.t[:, 8 + h:9 + h], None, ALU.mult),
                         reads=[S.r, misc.r], writes=[S.r])
                Gv = uR.t[:].bitcast(F32)
                for c_ in range(NCORES - 1):
                    P.dma("sp", Gv[:, 0:2048], coutL[c_ * 128:(c_ + 1) * 128, :], reads=[rcoutL], writes=[uR.r])
                    for h in range(8):
                        P.op("dve", lambda e: e.scalar_tensor_tensor(S.t[:, h, :], Gv[:, h * 256:(h + 1) * 256], misc.t[:, 16 + 8 * c_ + h:17 + 8 * c_ + h],
                                                                     S.t[:, h, :], ALU.mult, ALU.add),
                             reads=[uR.r, misc.r, S.r], writes=[S.r])
            if mixstop == "F":
                return
            P.op("act", lambda e: e.activation(Sbf.t[:], S.t[:], AF.Copy), reads=[S.r], writes=[Sbf.r])
            P.dma("sp", artS[n * 128:(n + 1) * 128, :], a_.t[:].rearrange("p k t -> p (k t)"), reads=[a_.r], writes=[Res("x")], key=a_.r)

    def branch_phase(st, l):
        def T(name, shape, dt):
            return Tile(st.enter_context(nc.sbuf_tensor(un("br_" + name), list(shape), dt)), P.res("br_" + name))
        art = [T("art%d" % i, [128, 24, 128], BF16) for i in range(2)]
        ga = [T("ga%d" % i, [128, 512], BF16) for i in range(2)]
        gr = [T("gr%d" % i, [128, 512], BF16) for i in range(2)]
        sa = [T("sa%d" % i, [128, 512], F32) for i in range(2)]
        sr = [T("sr%d" % i, [128, 512], F32) for i in range(2)]
        tm = [T("tm%d" % i, [128, 512], F32) for i in range(2)]
        mg = [T("mg%d" % i, [128, 512], BF16) for i in range(2)]
        mst = [T("mst%d" % i, [128, 4, 128], BF16) for i in range(2)]
        k = 0
        for s in range(4):
            wa, wr = w_get(2)
            for n in range(NCH):
                i = k % 2
                k += 1
                P.dma("sp", art[i].t[:].rearrange("p k t -> p (k t)"), artS[n * 128:(n + 1) * 128, :], writes=[art[i].r])
                P.dma("sp", ga[i].t[:], uS[n * 128:(n + 1) * 128, O_GA + s * 512:O_GA + (s + 1) * 512], writes=[ga[i].r])
                P.dma("sp", gr[i].t[:], uS[n * 128:(n + 1) * 128, O_GR + s * 512:O_GR + (s + 1) * 512], writes=[gr[i].r])
                pA, pR = pf[(2 * k) % 6], pf[(2 * k + 1) % 6]
                for kc in range(8):
                    P.op("pe", lambda e: e.matmul(pA.t[:], art[i].t[:, kc, :], wa.t[:, kc, :], start=(kc == 0), stop=(kc == 7)),
                         reads=[art[i].r, wa.r], writes=[pA.r], same_ok=True)
                for kc in range(16):
                    P.op("pe", lambda e: e.matmul(pR.t[:], art[i].t[:, 8 + kc, :], wr.t[:, kc, :], start=(kc == 0), stop=(kc == 15)),
                         reads=[art[i].r, wr.r], writes=[pR.r], same_ok=True)
                P.op("act", lambda e: e.activation(sa[i].t[:], ga[i].t[:], AF.Sigmoid), reads=[ga[i].r], writes=[sa[i].r])
                P.op("act", lambda e: e.activation(sr[i].t[:], gr[i].t[:], AF.Sigmoid), reads=[gr[i].r], writes=[sr[i].r])
                P.op("dve", lambda e: e.tensor_tensor(tm[i].t[:], sa[i].t[:], pA.t[:], ALU.mult), reads=[sa[i].r, pA.r], writes=[tm[i].r])
                P.op("dve", lambda e: e.tensor_tensor(sr[i].t[:], sr[i].t[:], pR.t[:], ALU.mult), reads=[sr[i].r, pR.r], writes=[sr[i].r])
                P.op("dve", lambda e: e.tensor_tensor(mg[i].t[:], tm[i].t[:], sr[i].t[:], ALU.add), reads=[tm[i].r, sr[i].r], writes=[mg[i].r])
                transposes_to(lambda lo, hi: (mst[i].t[:, lo:hi, :], mst[i].r), lambda b: mg[i].t[:, b * 128:(b + 1) * 128], mg[i].r, 4)
                P.dma("sp", mS[n * 128:(n + 1) * 128, s * 512:(s + 1) * 512], mst[i].t[:].rearrange("p k t -> p (k t)"),
                      reads=[mst[i].r], writes=[Res("x")], key=mst[i].r)

    def resid_gemm_phase(st, l, tag, lhs_loader, nk_list, hsrc):
        def T(name, shape, dt):
            return Tile(st.enter_context(nc.sbuf_tensor(un(tag + name), list(shape), dt)), P.res("rg_" + name))
        hsl = [T("hsl%d" % i, [128, 512], F32) for i in range(2)]
        ho = [T("ho%d" % i, [128, 512], F32) for i in range(2)]
        k = 0
        for s in range(4):
            blks = w_get(len(nk_list))
            for n in range(NCH):
                i = k % 2
                k += 1
                lt = lhs_loader(n)
                P.dma("sp", hsl[i].t[:], hsrc[n * 128:(n + 1) * 128, s * 512:(s + 1) * 512], writes=[hsl[i].r])
                ps = next_pf()
                tot = sum(nk_list)
                j = 0
                for bi, nk in enumerate(nk_list):
                    for kc in range(nk):
                        P.op("pe", lambda e: e.matmul(ps.t[:], lt.t[:, j, :], blks[bi].t[:, kc, :], start=(j == 0), stop=(j == tot - 1)),
                             reads=[lt.r, blks[bi].r], writes=[ps.r], same_ok=True)
                        j += 1
                P.op("dve", lambda e: e.tensor_tensor(ho[i].t[:], ps.t[:], hsl[i].t[:], ALU.add), reads=[ps.r, hsl[i].r], writes=[ho[i].r])
                P.dma("sp", hS[n * 128:(n + 1) * 128, s * 512:(s + 1) * 512], ho[i].t[:], reads=[ho[i].r], writes=[Res("x")], key=ho[i].r)
                chk("O%d" % (k + 1))

    def outproj_phase(st, l, hsrc):
        mT = [Tile(st.enter_context(nc.sbuf_tensor(un("op_mT%d" % n), [128, 16, 128], BF16)), P.res("xT%d" % n)) for n in range(NCH)]
        for n in range(NCH):
            P.dma("sp", mT[n].t[:].rearrange("p k t -> p (k t)"), mS[n * 128:(n + 1) * 128, :], writes=[mT[n].r])
        chk("O1")
        resid_gemm_phase(st, l, "op_", lambda n: mT[n], [16], hsrc)

    def ffn_up_phase(st, l, xT):
        def T(name, shape, dt):
            return Tile(st.enter_context(nc.sbuf_tensor(un("fu_" + name), list(shape), dt)), P.res("fu_" + name))
        sgt = [T("sg%d" % i, [128, 512], F32) for i in range(2)]
        hd = [T("hd%d" % i, [128, 512], BF16) for i in range(2)]
        hst = [T("hst%d" % i, [128, 4, 128], BF16) for i in range(2)]
        k = 0
        for j in range(11):
            wg, wu = w_get(2)
            for n in range(NCH):
                i = k % 2
                k += 1
                pG, pU = pf[(2 * k) % 6], pf[(2 * k + 1) % 6]
                for kc in range(16):
                    P.op("pe", lambda e: e.matmul(pG.t[:], xT[n].t[:, kc, :], wg.t[:, kc, :], start=(kc == 0), stop=(kc == 15)),
                         reads=[xT[n].r, wg.r], writes=[pG.r], same_ok=True)
                for kc in range(16):
                    P.op("pe", lambda e: e.matmul(pU.t[:], xT[n].t[:, kc, :], wu.t[:, kc, :], start=(kc == 0), stop=(kc == 15)),
                         reads=[xT[n].r, wu.r], writes=[pU.r], same_ok=True)
                P.op("act", lambda e: e.activation(sgt[i].t[:], pG.t[:], AF.Silu), reads=[pG.r], writes=[sgt[i].r])
                P.op("dve", lambda e: e.tensor_tensor(hd[i].t[:], sgt[i].t[:], pU.t[:], ALU.mult), reads=[sgt[i].r, pU.r], writes=[hd[i].r])
                transposes_to(lambda lo, hi: (hst[i].t[:, lo:hi, :], hst[i].r), lambda b: hd[i].t[:, b * 128:(b + 1) * 128], hd[i].r, 4)
                P.dma("sp", hidS[n * 128:(n + 1) * 128, j * 512:(j + 1) * 512], hst[i].t[:].rearrange("p k t -> p (k t)"),
                      reads=[hst[i].r], writes=[Res("x")], key=hst[i].r)

    def ffn_down_phase(st, l):
        hid = [Tile(st.enter_context(nc.sbuf_tensor(un("fd_hid%d" % i), [128, 44, 128], BF16)), P.res("fd_hid%d" % i)) for i in range(2)]
        cnt = [0]

        def loader(n):
            t_ = hid[cnt[0] % 2]
            cnt[0] += 1
            P.dma("sp", t_.t[:].rearrange("p k t -> p (k t)"), hidS[n * 128:(n + 1) * 128, :], writes=[t_.r])
            return t_
        resid_gemm_phase(st, l, "fd_", loader, [16, 16, 12], hS)

    def final_phase(st):
        gain = Tile(st.enter_context(nc.sbuf_tensor(un("fn_gain"), [128, D], F32)), P.res("gain"))
        P.dma("sp", gain.t[:], gains[4], writes=[gain.r])
        hb = [Tile(st.enter_context(nc.sbuf_tensor(un("fn_hb%d" % i), [128, D], F32)), P.res("hb%d" % i)) for i in range(2)]
        yo = [Tile(st.enter_context(nc.sbuf_tensor(un("fn_yo%d" % i), [128, D], F32)), P.res("yo%d" % i)) for i in range(2)]
        junk = Tile(st.enter_context(nc.sbuf_tensor(un("fn_junk"), [128, D], BF16)), P.res("junk"))
        stat = [Tile(st.enter_context(nc.sbuf_tensor(un("fn_stat%d" % i), [128, 4], F32)), P.res("stat%d" % i)) for i in range(2)]
        for n in range(1, NCH):
            h, s_, o_ = hb[n % 2], stat[n % 2], yo[n % 2]
            P.dma("sp", h.t[:], hS[n * 128:(n + 1) * 128, :], writes=[h.r])
            P.op("act", lambda e: e.activation(junk.t[:], h.t[:], AF.Square, accum_out=s_.t[:, 0:1]), reads=[h.r], writes=[junk.r, s_.r])
            P.op("act", lambda e: e.activation(s_.t[:, 1:2], s_.t[:, 0:1], AF.Sqrt, bias=EPS, scale=1.0 / D), reads=[s_.r], writes=[s_.r])
            P.op("dve", lambda e: e.reciprocal(s_.t[:, 2:3], s_.t[:, 1:2]), reads=[s_.r], writes=[s_.r])
            P.op("dve", lambda e: e.scalar_tensor_tensor(o_.t[:], h.t[:], s_.t[:, 2:3], gain.t[:], ALU.mult, ALU.mult),
                 reads=[h.r, s_.r, gain.r], writes=[o_.r])
            P.dma("sp", y[(n - 1) * 128:n * 128, :], o_.t[:], reads=[o_.r], writes=[Res("x")], key=o_.r)
        P.wait_all("sp", [t_.r for t_ in yo])

    def _body():
        hsrc = xin
        ph = 0
        for l in range(DEPTH):
            with ExitStack() as st:
                xT = norm_phase(st, hsrc, l, "n1_")
                inproj_phase(st, l, xT)
                P.barrier()
            ph += 1
            if ph >= stop:
                return nc
            with ExitStack() as st:
                evmode[0] = "act"
                mixer_phase(st, l)
                evmode[0] = "alt"
                P.barrier()
            ph += 1
            if ph >= stop:
                return nc
            with ExitStack() as st:
                branch_phase(st, l)
                P.barrier()
            ph += 1
            if ph >= stop:
                return nc
            with ExitStack() as st:
                outproj_phase(st, l, hsrc)
                P.barrier()
            ph += 1
            if ph >= stop:
                return nc
            hsrc = hS
            with ExitStack() as st:
                xT = norm_phase(st, hS, 2 + l, "n2_")
                ffn_up_phase(st, l, xT)
                P.barrier()
            ph += 1
            if ph >= stop:
                return nc
            with ExitStack() as st:
                ffn_down_phase(st, l)
                P.barrier()
            ph += 1
            if ph >= stop:
                return nc
        with ExitStack() as st:
            final_phase(st)
        return nc

    try:
        return _body()
    except StopBuild:
        P.barrier()
        return nc


def _t5_bucket(rel):
    n = np.maximum(rel, 0)
    nf = np.maximum(n, 1).astype(np.float32)
    large = 16 + (np.log(nf / np.float32(16)) / np.float32(math.log(128 / 16)) * np.float32(16)).astype(np.int32)
    large = np.minimum(large, 31)
    return np.where(n < 16, n, large)


def _prep(inputs, NOWN):
    NCH = NOWN + 1
    f32 = np.float32
    x = np.asarray(inputs["x"], f32)[0]
    meta = np.asarray(inputs["meta_tokens"], f32)
    rb = np.asarray(inputs["rel_bias"], f32)
    i = np.arange(128)
    j = np.arange(256)
    rel = i[:, None] + 128 - j[None, :]
    valid = (rel >= 0) & (rel < 128)
    bb = rb[_t5_bucket(rel)]
    bb = np.where(valid[:, :, None], bb, f32(NEGM)).transpose(0, 2, 1)
    m = np.arange(16)
    rel0 = i[:, None] - 112 - m[None, :]
    bm0 = np.where((rel0 >= 0)[:, :, None], rb[_t5_bucket(rel0)], f32(NEGM)).transpose(0, 2, 1)
    rel1 = 128 + i[:, None] - 112 - m[None, :]
    bm1 = rb[_t5_bucket(rel1)].transpose(0, 2, 1)
    bmc = np.broadcast_to(rb[31][None, :, None], (128, 16, 16))
    ld = np.log(1.0 - 2.0 ** (-5.0 - np.arange(8)))
    dect = np.zeros((128, 8, 128), f32)
    diff = i[None, :] - i[:, None]
    for h in range(8):
        dect[:, h, :] = np.where(diff >= 0, np.exp(ld[h] * np.maximum(diff, 0)), 0.0)
    qdec = np.broadcast_to(np.exp(ld[:, None] * (i[None, :] + 1.0))[None], (128, 8, 128)).astype(f32)
    kdec = np.exp(ld[None, :] * (127.0 - i[:, None])).astype(f32)
    cd = np.exp(ld * 128.0)
    angle = np.repeat((1.0 / (f32(10000.0) ** np.linspace(0.0, 1.0, 64, dtype=f32))).astype(f32), 2)
    sgn = np.tile(np.array([-1.0, 1.0]), 64)
    ksc = 128.0 ** -0.5
    g5 = np.stack([np.asarray(inputs["norm1"], f32)[0], np.asarray(inputs["norm1"], f32)[1],
                   np.asarray(inputs["norm2"], f32)[0], np.asarray(inputs["norm2"], f32)[1],
                   np.asarray(inputs["norm_f"], f32)], 0)
    gains = np.ascontiguousarray(np.broadcast_to(g5[:, None, :], (5, 128, D)))
    sinks = np.ascontiguousarray(np.broadcast_to(np.asarray(inputs["attn_sinks"], f32).reshape(1, 32), (128, 32)))
    shared = {
        "gains": gains, "sinks": sinks, "ident": np.eye(128, dtype=f32),
        "bband": np.ascontiguousarray(bb.reshape(128, 16 * 256)),
        "dect": dect.reshape(128, 1024), "qdec": np.ascontiguousarray(qdec.reshape(128, 1024)),
    }
    wflat = {}
    for nm in ("w_in", "w_attn_br", "w_ret_br", "w_out", "w_gate_up", "w_down"):
        w = np.asarray(inputs[nm], f32)
        wflat[nm] = w.reshape(w.shape[0] * w.shape[1], w.shape[2])
    in_maps = []
    for c in range(NCORES):
        xin = np.zeros((NCH * 128, D), f32)
        xin[112:128] = meta
        xin[128:] = x[c * NOWN * 128:(c + 1) * NOWN * 128]
        rot = np.zeros((NCH, 128, 512), f32)
        for n in range(NCH):
            g = 0 if n == 0 else 1 + c * NOWN + (n - 1)
            pos = (g * 128 + i - 112).astype(f32)
            theta = (pos[:, None] * angle[None, :]).astype(f32).astype(np.float64)
            cs, sn = np.cos(theta), np.sin(theta)
            rot[n, :, 0:128] = cs
            rot[n, :, 128:256] = sn * sgn[None, :]
            rot[n, :, 256:384] = cs * ksc
            rot[n, :, 384:512] = sn * sgn[None, :] * ksc
        misc = np.zeros((128, 128), f32)
        misc[:, 0:8] = kdec
        misc[:, 8:16] = (cd ** (NOWN * c))[None, :]
        for c2 in range(NCORES):
            if c2 < c:
                misc[:, 16 + 8 * c2:24 + 8 * c2] = (cd ** (NOWN * (c - 1 - c2)))[None, :]
        if c >= 1:
            misc[:, 80 + c - 1] = 1.0
        misc[:, 88] = NEGM if c == 0 else 0.0
        bmeta = np.stack([bm0, bm1 if c == 0 else bmc, bmc], 1)
        d = dict(shared)
        for nm in ("w_in", "w_attn_br", "w_ret_br", "w_out", "w_gate_up", "w_down"):
            w2 = wflat[nm]
            rws = w2.shape[0] // NCORES
            d[nm] = w2[c * rws:(c + 1) * rws]
        d.update({"xin": xin, "rot": rot.reshape(NCH * 128, 512), "misc": misc,
                  "bmeta": np.ascontiguousarray(bmeta.reshape(128, 3 * 256)).astype(f32)})
        in_maps.append(d)
    return in_maps


_NC_CACHE = {}


def run(inputs, NOWN, DEPTH=2, raw=False):
    key = (NOWN, DEPTH)
    if key not in _NC_CACHE:
        _NC_CACHE[key] = build(NOWN, DEPTH)
    nc = _NC_CACHE[key]
    in_maps = _prep(inputs, NOWN)
    res = run_bass_kernel_spmd(nc, in_maps, core_ids=list(range(NCORES)))
    if raw:
        return res.results
    out = np.concatenate([np.asarray(r["y"], np.float32) for r in res.results], axis=0)
    return out[None]


def kernel(**inputs):
    return run(inputs, 16, 2)
```
